# Optimizing a Trainium2 kernel written in Bass

```python
import jax, jax.numpy as jnp
from jax import lax
import numpy as np

D_MODEL = 1024
BATCH = 8
SEQ = 4096
DEPTH = 2

HEAD_DIM = 64
N_Q_HEADS = 8
N_KV_HEADS = 2
GQA_GROUP = N_Q_HEADS // N_KV_HEADS
WINDOW = 128
ROPE_THETA = 10000.0
RET_HEADS = 4
RET_QK_DIM = 128
RET_V_DIM = 2 * RET_QK_DIM
RET_CHUNK = 128
RET_THETA = 10000.0
D_FF = 4 * D_MODEL
EPS = 1e-6

ATT_Q = N_Q_HEADS * HEAD_DIM
ATT_KV = N_KV_HEADS * HEAD_DIM
RET_QK = RET_HEADS * RET_QK_DIM
RET_V = RET_HEADS * RET_V_DIM
SPLIT_SIZES = (ATT_Q, ATT_KV, ATT_KV, RET_QK, RET_QK, RET_V, RET_V, D_MODEL, D_MODEL)
W_IN = sum(SPLIT_SIZES)
SPLIT_POINTS = tuple(int(v) for v in np.cumsum(SPLIT_SIZES)[:-1])

kernel_name = "hybrid_swa_sink_retention_gated"


def rmsnorm(x, g):
    xf = x.astype(jnp.float32)
    y = xf * lax.rsqrt(jnp.mean(xf * xf, axis=-1, keepdims=True) + EPS)
    return (y * g.astype(jnp.float32)).astype(x.dtype)


def rope_half(x, pos):
    half = x.shape[-1] // 2
    inv = ROPE_THETA ** (-jnp.arange(half, dtype=jnp.float32) / half)
    ang = pos.astype(jnp.float32)[:, None] * inv[None, :]
    cos = jnp.cos(ang)[None, :, None, :]
    sin = jnp.sin(ang)[None, :, None, :]
    xf = x.astype(jnp.float32)
    x1, x2 = xf[..., :half], xf[..., half:]
    return jnp.concatenate([x1 * cos - x2 * sin, x2 * cos + x1 * sin], axis=-1).astype(x.dtype)


def rotate_every_two(x):
    x1, x2 = x[..., ::2], x[..., 1::2]
    return jnp.stack([-x2, x1], axis=-1).reshape(x.shape)


def retention_rotation(x, pos):
    dk = x.shape[-1]
    theta = 1.0 / (RET_THETA ** jnp.linspace(0.0, 1.0, dk // 2, dtype=jnp.float32))
    theta = jnp.repeat(theta, 2)
    ang = pos.astype(jnp.float32)[:, None] * theta[None, :]
    cos = jnp.cos(ang)[None, :, None, :]
    sin = jnp.sin(ang)[None, :, None, :]
    xf = x.astype(jnp.float32)
    return (xf * cos + rotate_every_two(xf) * sin).astype(x.dtype)


def sliding_window_sink_attention(q, k, v, sinks):
    B, S = q.shape[0], q.shape[1]
    C = WINDOW
    nb = S // C
    qb = q.reshape(B, nb, C, N_KV_HEADS, GQA_GROUP, HEAD_DIM)
    kb = k.reshape(B, nb, C, N_KV_HEADS, HEAD_DIM)
    vb = v.reshape(B, nb, C, N_KV_HEADS, HEAD_DIM)

    def band(t):
        prev = jnp.pad(t, ((0, 0), (1, 0), (0, 0), (0, 0), (0, 0)))[:, :-1]
        return jnp.concatenate([prev, t], axis=2)

    kk, vv = band(kb), band(vb)
    s = jnp.einsum('bnqhgd,bnkhd->bnhgqk', qb, kk).astype(jnp.float32) * (HEAD_DIM ** -0.5)
    qi = jnp.arange(C)[:, None] + C
    kj = jnp.arange(2 * C)[None, :]
    rel = qi - kj
    valid = (rel >= 0) & (rel < WINDOW)
    first = (jnp.arange(nb) > 0)[:, None, None] | (kj >= C)[None]
    mask = valid[None] & first
    s = jnp.where(mask[None, :, None, None], s, -jnp.inf)
    sink = jnp.broadcast_to(
        sinks.astype(jnp.float32).reshape(1, 1, N_KV_HEADS, GQA_GROUP, 1, 1),
        s.shape[:-1] + (1,))
    p = jax.nn.softmax(jnp.concatenate([s, sink], axis=-1), axis=-1)[..., :-1]
    o = jnp.einsum('bnhgqk,bnkhd->bnqhgd', p.astype(v.dtype), vv)
    return o.reshape(B, S, ATT_Q)


def chunkwise_retention(q, k, v):
    B, S = q.shape[0], q.shape[1]
    C = RET_CHUNK
    nc = S // C
    log_g = jnp.log(1.0 - 2.0 ** (-5.0 - jnp.arange(RET_HEADS, dtype=jnp.float32)))
    idx = jnp.arange(C, dtype=jnp.float32)
    rel = idx[:, None] - idx[None, :]
    dmask = jnp.where(rel[None] >= 0, jnp.exp(log_g[:, None, None] * jnp.maximum(rel, 0.0)[None]), 0.0)

    qc = q.astype(jnp.float32).reshape(B, nc, C, RET_HEADS, RET_QK_DIM)
    kc = k.astype(jnp.float32).reshape(B, nc, C, RET_HEADS, RET_QK_DIM)
    vc = v.astype(jnp.float32).reshape(B, nc, C, RET_HEADS, RET_V_DIM)

    s = jnp.einsum('bnihd,bnjhd->bnhij', qc, kc) * dmask[None, None]
    inner = jnp.einsum('bnhij,bnjhe->bnihe', s, vc)

    zeta = jnp.exp(log_g[None, :] * (C - 1.0 - idx)[:, None])
    kv = jnp.einsum('bnjhd,bnjhe->nbhde', kc, vc * zeta[None, None, :, :, None])
    g_chunk = jnp.exp(log_g * C)[None, :, None, None]

    def step(R, kv_n):
        return R * g_chunk + kv_n, R

    R0 = jnp.zeros((B, RET_HEADS, RET_QK_DIM, RET_V_DIM), jnp.float32)
    _, R_prev = lax.scan(step, R0, kv)
    xi = jnp.exp(log_g[None, :] * (idx + 1.0)[:, None])
    cross = jnp.einsum('bnihd,nbhde->bnihe', qc, R_prev) * xi[None, None, :, :, None]
    return (inner + cross).reshape(B, S, RET_HEADS, RET_V_DIM)


def hybrid_layer(x, g_mix, w_in, sinks, w_a, w_b, w_out, g_mlp, w_up, w_down):
    B, S, _ = x.shape
    pos = jnp.arange(S)
    h = rmsnorm(x, g_mix)
    z = h @ w_in
    aq, ak, av, rq, rk, rv, rg, ga, gb = jnp.split(z, SPLIT_POINTS, axis=-1)

    aq = rope_half(aq.reshape(B, S, N_Q_HEADS, HEAD_DIM), pos)
    ak = rope_half(ak.reshape(B, S, N_KV_HEADS, HEAD_DIM), pos)
    av = av.reshape(B, S, N_KV_HEADS, HEAD_DIM)
    ya = sliding_window_sink_attention(aq, ak, av, sinks) @ w_a

    rq = retention_rotation(rq.reshape(B, S, RET_HEADS, RET_QK_DIM), pos)
    rk = retention_rotation(rk.reshape(B, S, RET_HEADS, RET_QK_DIM), pos) * (RET_QK_DIM ** -0.5)
    rv = rv.reshape(B, S, RET_HEADS, RET_V_DIM)
    r = chunkwise_retention(rq, rk, rv)
    r = r * lax.rsqrt(jnp.mean(r * r, axis=-1, keepdims=True) + EPS)
    r = r.reshape(B, S, RET_V) * jax.nn.silu(rg.astype(jnp.float32))
    yb = r.astype(x.dtype) @ w_b

    mixed = jax.nn.sigmoid(ga) * ya + jax.nn.sigmoid(gb) * yb
    x = x + mixed @ w_out

    h2 = rmsnorm(x, g_mlp)
    x = x + jnp.square(jax.nn.relu(h2 @ w_up)) @ w_down
    return x


def setup_inputs(seed: int = 0) -> dict:
    key = jax.random.key(seed)
    ks = jax.random.split(key, 12)
    f32 = jnp.float32

    def nrm(k, shape, fan_in):
        return jax.random.normal(k, shape, f32) * (fan_in ** -0.5)

    return {
        "x": jax.random.normal(ks[0], (BATCH, SEQ, D_MODEL), f32),
        "g_mix": 1.0 + 0.05 * jax.random.normal(ks[1], (DEPTH, D_MODEL), f32),
        "w_in": nrm(ks[2], (DEPTH, D_MODEL, W_IN), D_MODEL),
        "sinks": 0.5 * jax.random.normal(ks[3], (DEPTH, N_Q_HEADS), f32),
        "w_a": nrm(ks[4], (DEPTH, ATT_Q, D_MODEL), ATT_Q),
        "w_b": nrm(ks[5], (DEPTH, RET_V, D_MODEL), RET_V),
        "w_out": nrm(ks[6], (DEPTH, D_MODEL, D_MODEL), D_MODEL),
        "g_mlp": 1.0 + 0.05 * jax.random.normal(ks[7], (DEPTH, D_MODEL), f32),
        "w_up": nrm(ks[8], (DEPTH, D_MODEL, D_FF), D_MODEL),
        "w_down": nrm(ks[9], (DEPTH, D_FF, D_MODEL), D_FF),
        "g_final": 1.0 + 0.05 * jax.random.normal(ks[10], (D_MODEL,), f32),
    }


def reference(x, g_mix, w_in, sinks, w_a, w_b, w_out, g_mlp, w_up, w_down, g_final):
    for l in range(DEPTH):
        x = hybrid_layer(x, g_mix[l], w_in[l], sinks[l], w_a[l], w_b[l], w_out[l],
                         g_mlp[l], w_up[l], w_down[l])
    return rmsnorm(x, g_final)
```

```python
import os
import numpy as np
from contextlib import ExitStack
import concourse.bass as bass
import concourse.mybir as mybir
from concourse.bass_utils import run_bass_kernel_spmd

F32 = mybir.dt.float32
BF16 = mybir.dt.bfloat16
AF = mybir.ActivationFunctionType
SIGF = getattr(AF, os.environ.get('KDBG_SIG', 'Sigmoid'))
ALU = mybir.AluOpType

D = 1024
NT = 8
TBLK = 1024
EPS = 1e-6
ENGINES = ("tensor", "scalar", "vector", "gpsimd", "sync")


class Op:
    __slots__ = ("eng", "fn", "reads", "writes", "dma_sem", "ndma", "idx", "deps",
                 "need_sig", "count", "sem", "name", "epoch")

    def __init__(self, eng, fn, reads, writes, dma_sem=None, ndma=0, name="", epoch=0):
        self.eng = eng
        self.fn = fn
        self.reads = reads
        self.writes = writes
        self.dma_sem = dma_sem
        self.ndma = ndma
        self.deps = []
        self.need_sig = False
        self.count = None
        self.sem = None
        self.name = name
        self.epoch = epoch


class Prog:
    def __init__(self, nc):
        self.nc = nc
        self.ops = []
        self.last_writer = {}
        self.readers = {}
        self.dma_counts = {}
        self.epoch = 0

    def add(self, eng, fn, reads=(), writes=(), name=""):
        reads = list(reads)
        writes = list(writes)
        for k in reads:
            if isinstance(k, tuple) and k[0] in ("pf", "pb") and k not in writes:
                writes.append(k)
        op = Op(eng, fn, reads, writes, name=name, epoch=self.epoch)
        self._track(op)
        return op

    def dma(self, eng, fn, sem_name, ndma, reads=(), writes=(), name=""):
        op = Op(eng, fn, list(reads), list(writes), dma_sem=sem_name, ndma=ndma, name=name,
                epoch=self.epoch)
        self._track(op)
        return op

    def _track(self, op):
        op.idx = len(self.ops)
        deps = {}
        for k in op.reads:
            w = self.last_writer.get(k)
            if w is not None:
                deps[w.idx] = w
        for k in op.writes:
            w = self.last_writer.get(k)
            if w is not None:
                deps[w.idx] = w
            for r in self.readers.get(k, ()):
                deps[r.idx] = r
        for k in op.reads:
            self.readers.setdefault(k, []).append(op)
        for k in op.writes:
            self.last_writer[k] = op
            self.readers[k] = []
        deps.pop(op.idx, None)
        op.deps = list(deps.values())
        self.ops.append(op)

    def sem_names(self):
        names = set()
        for op in self.ops:
            if op.dma_sem is not None:
                names.add(op.dma_sem)
            else:
                names.add("e_%s_%d" % (op.eng, op.epoch))
        return sorted(names)

    def emit(self, sems):
        nc = self.nc

        def skip(d, op):
            return (d.dma_sem is None and op.dma_sem is None and d.eng == "tensor" and op.eng == "tensor")

        for op in self.ops:
            for d in op.deps:
                if d.dma_sem is None and not skip(d, op):
                    d.need_sig = True
        counters = {}
        for op in self.ops:
            if op.dma_sem is not None:
                c = self.dma_counts.get(op.dma_sem, 0) + 16 * op.ndma
                self.dma_counts[op.dma_sem] = c
                op.count = c
                op.sem = op.dma_sem
            elif op.need_sig:
                s = "e_%s_%d" % (op.eng, op.epoch)
                counters[s] = counters.get(s, 0) + 1
                op.count = counters[s]
                op.sem = s
        self.max_counts = counters
        per_eng = {e: [] for e in ENGINES}
        for op in self.ops:
            per_eng[op.eng].append(op)
        stats = {e: 0 for e in ENGINES}

        def run_engine(ename, eng):
            waited = {}
            for op in per_eng[ename]:
                need = {}
                for d in op.deps:
                    if d.count is None or skip(d, op):
                        continue
                    if need.get(d.sem, 0) < d.count:
                        need[d.sem] = d.count
                for s, v in need.items():
                    if waited.get(s, 0) >= v:
                        continue
                    eng.wait_ge(sems[s], v)
                    waited[s] = v
                    stats[ename] += 1
                r = op.fn(eng)
                if op.dma_sem is not None:
                    assert len(r) == op.ndma, (op.name, len(r), op.ndma)
                    for ins in r:
                        ins.then_inc(sems[op.dma_sem], 16)
                elif op.need_sig:
                    assert r is not None, op.name
                    r.then_inc(sems[op.sem], 1)

        with nc.Block() as block:
            @block.tensor
            def _(e):
                run_engine("tensor", e)

            @block.scalar
            def _(e):
                run_engine("scalar", e)

            @block.vector
            def _(e):
                run_engine("vector", e)

            @block.gpsimd
            def _(e):
                run_engine("gpsimd", e)

            @block.sync
            def _(e):
                run_engine("sync", e)
        self.wait_stats = stats


ATT_Q, ATT_KV, RET_QK, RET_V = 512, 128, 512, 1024
OFF_AQ = 0
OFF_AK = 512
OFF_AV = 640
OFF_RQ = 768
OFF_RK = 1280
OFF_RV = 1792
OFF_RG = 2816
OFF_GA = 3840
OFF_GB = 4864

CHUNKS = [("A1", 8, 512), ("A2", 8, 384), ("GA1", 8, 512), ("GA2", 8, 512), ("WA", 4, 1024)]
for _h in range(4):
    CHUNKS += [("RA%d" % _h, 8, 512), ("RB%d" % _h, 8, 256)]
CHUNKS += [("GB1", 8, 512), ("GB2", 8, 512), ("WB1", 8, 512), ("WB2", 8, 512), ("WO1", 8, 512), ("WO2", 8, 512)]
for _q in range(4):
    CHUNKS += [("UP%da" % _q, 8, 512), ("UP%db" % _q, 8, 512), ("DN%da" % _q, 8, 512), ("DN%db" % _q, 8, 512)]
CH_SIZE = [kc * n for (_, kc, n) in CHUNKS]
CH_OFF = [int(v) for v in np.concatenate([[0], np.cumsum(CH_SIZE)[:-1]])]
CH_IDX = {name: i for i, (name, _, _) in enumerate(CHUNKS)}
WTOT = int(sum(CH_SIZE))
NCH = len(CHUNKS)

C_ID, C_MGT, C_MLE, C_GMIX, C_GMLP, C_SINK, C_DQ, C_NDQ, C_DK, C_NDK, C_GF = 0, 128, 256, 384, 400, 416, 432, 436, 440, 444, 448
NCONST = 448 + 1024
NTAB = 192


def _chunk(W, cols, kc_n):
    sub = W[:, cols]
    return np.ascontiguousarray(sub.reshape(kc_n, 128, len(cols)).transpose(1, 0, 2)).reshape(128, -1)


def prep_weights(w_in, w_a, w_b, w_out, w_up, w_down):
    L = w_in.shape[0]
    out = np.empty((L, 128, WTOT), np.float32)
    perm = np.concatenate([np.arange(0, 128, 2), np.arange(1, 128, 2)])
    ar = np.arange
    for l in range(L):
        parts = {}
        parts["A1"] = _chunk(w_in[l], OFF_AQ + ar(512), 8)
        kcols = np.concatenate([OFF_AK + ar(64), OFF_AK + ar(64), OFF_AK + 64 + ar(64), OFF_AK + 64 + ar(64),
                                OFF_AV + ar(128)])
        parts["A2"] = _chunk(w_in[l], kcols, 8)
        parts["GA1"] = _chunk(w_in[l], OFF_GA + ar(512), 8)
        parts["GA2"] = _chunk(w_in[l], OFF_GA + 512 + ar(512), 8)
        parts["WA"] = _chunk(w_a[l], ar(1024), 4)
        for h in range(4):
            cols = np.concatenate([OFF_RQ + h * 128 + perm, OFF_RK + h * 128 + perm, OFF_RV + h * 256 + ar(256)])
            parts["RA%d" % h] = _chunk(w_in[l], cols, 8)
            parts["RB%d" % h] = _chunk(w_in[l], OFF_RG + h * 256 + ar(256), 8)
        parts["GB1"] = _chunk(w_in[l], OFF_GB + ar(512), 8)
        parts["GB2"] = _chunk(w_in[l], OFF_GB + 512 + ar(512), 8)
        parts["WB1"] = _chunk(w_b[l], ar(512), 8)
        parts["WB2"] = _chunk(w_b[l], 512 + ar(512), 8)
        parts["WO1"] = _chunk(w_out[l], ar(512), 8)
        parts["WO2"] = _chunk(w_out[l], 512 + ar(512), 8)
        for q in range(4):
            parts["UP%da" % q] = _chunk(w_up[l], q * 1024 + ar(512), 8)
            parts["UP%db" % q] = _chunk(w_up[l], q * 1024 + 512 + ar(512), 8)
            parts["DN%da" % q] = _chunk(w_down[l][q * 1024:(q + 1) * 1024], ar(512), 8)
            parts["DN%db" % q] = _chunk(w_down[l][q * 1024:(q + 1) * 1024], 512 + ar(512), 8)
        for i, (name, kc, n) in enumerate(CHUNKS):
            out[l, :, CH_OFF[i]:CH_OFF[i] + CH_SIZE[i]] = parts[name]
    return out


def prep_consts(g_mix, sinks, g_mlp, g_final):
    c = np.zeros((128, NCONST), np.float32)
    c[:, C_ID:C_ID + 128] = np.eye(128, dtype=np.float32)
    k = np.arange(128)[:, None]
    q = np.arange(128)[None, :]
    c[:, C_MGT:C_MGT + 128] = (k > q).astype(np.float32)
    c[:, C_MLE:C_MLE + 128] = (k <= q).astype(np.float32)
    L = g_mix.shape[0]
    for l in range(L):
        c[:, C_GMIX + l * 8:C_GMIX + l * 8 + 8] = g_mix[l].reshape(8, 128).T
        c[:, C_GMLP + l * 8:C_GMLP + l * 8 + 8] = g_mlp[l].reshape(8, 128).T
        for g in range(2):
            for half in range(2):
                for cc in range(2):
                    head = 4 * g + 2 * cc + half
                    c[:, C_SINK + l * 8 + g * 4 + half * 2 + cc] = sinks[l, head]
    i = np.arange(128, dtype=np.float64)
    for h in range(4):
        gam = 1.0 - 2.0 ** (-5.0 - h)
        dq = gam ** (i + 1.0)
        dk = gam ** (-(i + 1.0)) * (128.0 ** -0.5)
        c[:, C_DQ + h] = dq
        c[:, C_NDQ + h] = -dq
        c[:, C_DK + h] = dk
        c[:, C_NDK + h] = -dk
    c[:, C_GF:C_GF + 1024] = g_final[None, :]
    return c


def prep_tabs(S=4096):
    pos = np.arange(S, dtype=np.float32)
    inv = (10000.0 ** (-np.arange(32, dtype=np.float32) / 32)).astype(np.float32)
    angA = pos[:, None] * inv[None, :]
    theta = (1.0 / (10000.0 ** np.linspace(0.0, 1.0, 64, dtype=np.float32))).astype(np.float32)
    angR = pos[:, None] * theta[None, :]
    t = np.concatenate([np.cos(angA), np.sin(angA), np.cos(angR), np.sin(angR)], axis=1)
    return np.ascontiguousarray(t.astype(np.float32))


GAM128 = [float((1.0 - 2.0 ** (-5.0 - h)) ** 128) for h in range(4)]


def build_program(nblocks=4, nlayers=2, stop=None, dumps=()):
    nc = bass.Bass("TRN2", target_bir_lowering=False)
    S = nblocks * TBLK
    x_d = nc.dram_tensor("x", [S, D], F32, kind="ExternalInput").ap()
    w_d = nc.dram_tensor("wts", [2, 128, WTOT], F32, kind="ExternalInput").ap()
    c_d = nc.dram_tensor("consts", [128, NCONST], F32, kind="ExternalInput").ap()
    t_d = nc.dram_tensor("tabs", [4096, NTAB], F32, kind="ExternalInput").ap()
    o_d = nc.dram_tensor("out", [S, D], F32, kind="ExternalOutput").ap()
    dbg_d = {}
    for nm in dumps:
        dbg_d[nm] = nc.dram_tensor("dbg_" + nm, [128, 8192], F32, kind="ExternalOutput").ap()

    es = ExitStack()

    def sb(name, shape, dt):
        return es.enter_context(nc.sbuf_tensor("s_" + name, shape, dt))

    cst = sb("cst", [128, NCONST], F32)
    identb = sb("identb", [128, 128], BF16)
    mask2b = sb("mask2b", [128, 2, 128], BF16)
    esink = sb("esink", [128, 16], F32)
    mhalf = sb("mhalf", [128, 8], F32)
    dummy = sb("dummyk", [128, 8], F32)
    xres = sb("xres", [128, NT, D], F32)
    hT = sb("hT", [128, 8, TBLK], BF16)
    wsl = sb("wsl", [128, 4, 4096], BF16)
    tabs = sb("tabs", [128, NT, NTAB], F32)
    xn = sb("xn", [128, 2, D], BF16)
    junk = sb("junk", [128, D], BF16)
    scr = sb("scr", [128, 2, 512], F32)
    qrope = sb("qrope", [128, 2, 512], BF16)
    krope = sb("krope", [128, 2, 256], BF16)
    g1 = sb("g1", [128, 14336], BF16)
    KTs = sb("KTs", [128, 2, 2, 128], BF16)
    Vs = sb("Vs", [128, 2, 2, 65], BF16)
    Ebuf = sb("Ebuf", [128, 2, 2, 512], BF16)
    Atok = sb("Atok", [128, 2, 512], BF16)
    small = sb("small", [128, 64], F32)
    sgu = sb("sgu", [128, 8, TBLK], BF16)
    m1T = sb("m1T", [128, 8, TBLK], BF16)
    qtok = sb("qtok", [128, 2, 128], BF16)
    Smb = sb("Smb", [128, 2, 128], BF16)
    rtok = sb("rtok", [128, 2, 256], BF16)
    Z = sb("Z", [128, 2, 4, 256], F32)
    Rb = sb("Rb", [128, 2, 256], BF16)
    rT = sb("rT", [128, 8, TBLK], BF16)
    dstage = sb("dstage", [128, 1024], F32) if dumps else None
    psf = es.enter_context(nc.psum_tensor("psf", [128, 6, 512], F32))
    psb = es.enter_context(nc.psum_tensor("psb", [128, 2, 1024], BF16))

    QT = g1[:, 0:4096].rearrange("p (c t) -> p c t", c=4)
    AT = g1[:, 4096:8192].rearrange("p (c t) -> p c t", c=4)
    KT = g1[:, 8192:10240].rearrange("p (g t) -> p g t", g=2)
    Vaug = g1[:, 10240:10240 + 8 * 2 * 65].rearrange("p (t g e) -> p t g e", t=8, g=2)
    qkT = g1[:, 0:4096].rearrange("p (a b t) -> p a b t", a=2, b=2)
    ktok = g1[:, 4096:6144].rearrange("p (a t d) -> p a t d", a=2, t=8)
    vtok = g1[:, 6144:10240].rearrange("p (a t d) -> p a t d", a=2, t=8)
    gate = g1[:, 10240:14336].rearrange("p (a t d) -> p a t d", a=2, t=8)
    G1K = ("g1",)

    P = Prog(nc)

    def op(eng, method, reads, writes, name="", **kw):
        return P.add(eng, lambda e: getattr(e, method)(**kw), reads, writes, name)

    def mm(items, reads, writes, name=""):
        def fn(e):
            ins = None
            for (o, l, r, st, sp) in items:
                ins = e.matmul(o, l, r, start=st, stop=sp)
            return ins
        return P.add("tensor", fn, reads, writes, name)

    def tr(items, reads, writes, name=""):
        def fn(e):
            ins = None
            for (o, i_) in items:
                ins = e.transpose(o, i_, identb[:])
            return ins
        return P.add("tensor", fn, reads + ["identb"], writes, name)

    st = {"pf": 0, "pb": 0, "sm": 0, "scr": 0}

    def pf():
        i = st["pf"]
        st["pf"] = (i + 1) % 6
        return psf[:, i, :], ("pf", i)

    def pb():
        i = st["pb"]
        st["pb"] = (i + 1) % 2
        return psb[:, i, :], ("pb", i)

    def sm(n):
        i = st["sm"]
        if i + n > 64:
            i = 0
        st["sm"] = i + n
        return small[:, i:i + n], [("sm", j) for j in range(i, i + n)]

    wstate = {"next_load": 0, "seq": []}

    def w_plan(nb, nl):
        seq = []
        for b in range(nb):
            for l in range(nl):
                for ci in range(NCH):
                    seq.append((l, ci))
        return seq

    wstate["seq"] = w_plan(nblocks, nlayers)

    def w_issue():
        n = wstate["next_load"]
        if n >= len(wstate["seq"]):
            return
        wstate["next_load"] = n + 1
        l, ci = wstate["seq"][n]
        slot = n % 4
        size = CH_SIZE[ci]
        src = w_d[l, :, CH_OFF[ci]:CH_OFF[ci] + size]
        dst = wsl[:, slot, 0:size]
        P.dma("gpsimd", lambda e: [e.dma_start(out=dst, in_=src, max_dma_last_dim=4096)], "w%d" % slot, 1,
              writes=[("w", slot)], name="wload%d" % n)

    wuse = {"n": 0}

    def w_get(name, layer):
        n = wuse["n"]
        l, ci = wstate["seq"][n]
        assert CHUNKS[ci][0] == name and l == layer, (CHUNKS[ci][0], name, l, layer)
        wuse["n"] = n + 1
        slot = n % 4
        kc, ncols = CHUNKS[ci][1], CHUNKS[ci][2]
        view = wsl[:, slot, 0:kc * ncols].rearrange("p (k n) -> p k n", k=kc)
        return view, ("w", slot)

    def w_done():
        w_issue()

    P.dma("sync", lambda e: [e.dma_start(out=cst[:], in_=c_d[:, :])], "cstl", 1, writes=["cst"])
    op("vector", "tensor_copy", ["cst"], ["identb"], out=identb[:], in_=cst[:, C_ID:C_ID + 128])
    op("vector", "tensor_copy", ["cst"], ["mask2b"], out=mask2b[:, 0, :], in_=cst[:, C_MGT:C_MGT + 128])
    op("vector", "tensor_copy", ["cst", "mask2b"], ["mask2b"], out=mask2b[:, 1, :], in_=cst[:, C_MLE:C_MLE + 128])
    op("scalar", "activation", ["cst"], ["esink"], out=esink[:], in_=cst[:, C_SINK:C_SINK + 16], func=AF.Exp)
    op("vector", "memset", [], [("Z", l, h) for l in range(2) for h in range(4)], ap=Z[:], constant=0.0)
    op("gpsimd", "memset", [], ["mhalf"], ap=mhalf[:], constant=-0.5)
    for _ in range(4):
        w_issue()

    xkeys = lambda t: [("x", t, 0), ("x", t, 1)]

    def rstd_from_ms(ms_ap, ms_keys, n=1):
        v, vk = sm(n)
        r, rk = sm(n)
        op("gpsimd", "tensor_scalar", ms_keys, vk, out=v, in0=ms_ap, scalar1=EPS, scalar2=None, op0=ALU.add)
        op("gpsimd", "tensor_tensor", vk + ["mhalf"], rk, out=r, in0=v, in1=mhalf[:, 0:n], op=ALU.pow)
        return r, rk

    def norm_phase(l, gcol):
        for t in range(NT):
            ms, msk = sm(1)
            op("scalar", "activation", xkeys(t), msk, out=junk[:], in_=xres[:, t, :], func=AF.Square,
               scale=1.0 / 32.0, accum_out=ms)
            r, rk = rstd_from_ms(ms, msk)
            par = t % 2
            op("vector", "tensor_scalar", xkeys(t) + rk, [("xn", par)], out=xn[:, par, :], in0=xres[:, t, :],
               scalar1=r, scalar2=None, op0=ALU.mult)
            bank, bk = pb()
            tr([(bank[:, kc * 128:(kc + 1) * 128], xn[:, par, kc * 128:(kc + 1) * 128]) for kc in range(8)],
               [("xn", par)], [bk])
            gb = cst[:, gcol:gcol + 8].unsqueeze(2).broadcast_to([128, 8, 128])
            op("vector", "tensor_tensor", [bk, "cst"], [("hT", t)], out=hT[:, :, t * 128:(t + 1) * 128],
               in0=bank.rearrange("p (k t) -> p k t", k=8), in1=gb, op=ALU.mult)

    def rope(xa, nh, half, cosb, sinb, s_pos, s_neg, out_ap, rk, wk, pre_reads, scr_off):
        n = nh * 2 * half
        u = scr[:, 0, scr_off:scr_off + n]
        tt = scr[:, 1, scr_off:scr_off + n]
        xv = xa.rearrange("p (h two d) -> p h two d", two=2, d=half)
        uv = u.rearrange("p (h two d) -> p h two d", two=2, d=half)
        tv = tt.rearrange("p (h two d) -> p h two d", two=2, d=half)
        cb = cosb.unsqueeze(1).broadcast_to([128, nh, half])
        sbb = sinb.unsqueeze(1).broadcast_to([128, nh, half])
        offs = [o for o in (0, 128) if scr_off <= o < scr_off + n or (o == 0 and scr_off == 0)]
        ku = [("scr", 0, o) for o in offs]
        kt_ = [("scr", 1, o) for o in offs]
        for j in range(2):
            op("vector", "scalar_tensor_tensor", pre_reads + rk + ku, ku, out=uv[:, :, j, :], in0=xv[:, :, j, :],
               scalar=s_pos, in1=cb, op0=ALU.mult, op1=ALU.mult)
        op("vector", "scalar_tensor_tensor", pre_reads + rk + kt_, kt_, out=tv[:, :, 0, :], in0=xv[:, :, 1, :],
           scalar=s_neg, in1=sbb, op0=ALU.mult, op1=ALU.mult)
        op("vector", "scalar_tensor_tensor", pre_reads + rk + kt_, kt_, out=tv[:, :, 1, :], in0=xv[:, :, 0, :],
           scalar=s_pos, in1=sbb, op0=ALU.mult, op1=ALU.mult)
        op("vector", "tensor_tensor", ku + kt_, wk, out=out_ap, in0=u, in1=tt, op=ALU.add)

    def barrier_g1():
        op("vector", "memset", [], [G1K], ap=dummy[:, 0:1], constant=0.0)

    def dump(name, ap, keys, ncols):
        if name not in dbg_d:
            return
        op("vector", "tensor_copy", keys + ["dstage"], ["dstage"], out=dstage[:, 0:ncols] if len(ap.shape) == 2 else
           dstage[:, 0:ncols].rearrange("p (a b) -> p a b", a=ap.shape[1]), in_=ap)
        dd = dbg_d[name]
        P.dma("sync", lambda e: [e.dma_start(out=dd[:, 0:ncols], in_=dstage[:, 0:ncols])], "dbgs", 1,
              reads=["dstage"], writes=[("dbg", name)])

    def dump_big(name, tens, keys):
        if name not in dbg_d:
            return
        dd = dbg_d[name]
        for c in range(8):
            op("vector", "tensor_copy", keys + ["dstage"], ["dstage"], out=dstage[:, :], in_=tens[:, c, :])
            P.dma("sync", lambda e, c=c: [e.dma_start(out=dd[:, c * 1024:(c + 1) * 1024], in_=dstage[:, :])], "dbgs", 1,
                  reads=["dstage"], writes=[("dbg", name, c)])

    def layer(b, l, is_last_layer):
        first_blk = (b == 0)
        norm_phase(l, C_GMIX + l * 8)
        if stop == (l, "P1"):
            return True
        barrier_g1()
        op("vector", "memset", [G1K], [("V", t) for t in range(NT)], ap=Vaug[:, :, :, 64:65], constant=1.0)
        wA1, kA1 = w_get("A1", l)
        for t in range(NT):
            bank, bk = pf()
            mm([(bank[:, 0:512], hT[:, kc, t * 128:(t + 1) * 128], wA1[:, kc, :], kc == 0, kc == 7) for kc in range(8)],
               [("hT", t), kA1], [bk])
            par = t % 2
            rope(bank[:, 0:512], 8, 32, tabs[:, t, 0:32], tabs[:, t, 32:64], 0.125, -0.125, qrope[:, par, :],
                 [bk, "tabs"], [("qrope", par)], [], 0)
            tb, tbk = pb()
            tr([(tb[:, c * 128:(c + 1) * 128], qrope[:, par, c * 128:(c + 1) * 128]) for c in range(4)],
               [("qrope", par)], [tbk])
            op("scalar", "activation", [tbk, G1K], [("QT", t)], out=QT[:, :, t * 128:(t + 1) * 128],
               in_=tb[:, 0:512].rearrange("p (c t) -> p c t", c=4), func=AF.Copy)
        w_done()
        wA2, kA2 = w_get("A2", l)
        for t in range(NT):
            bank, bk = pf()
            mm([(bank[:, 0:384], hT[:, kc, t * 128:(t + 1) * 128], wA2[:, kc, :], kc == 0, kc == 7) for kc in range(8)],
               [("hT", t), kA2], [bk])
            par = t % 2
            rope(bank[:, 0:256], 4, 32, tabs[:, t, 0:32], tabs[:, t, 32:64], 1.0, -1.0, krope[:, par, :],
                 [bk, "tabs"], [("krope", par)], [], 0)
            tb, tbk = pb()
            tr([(tb[:, g * 128:(g + 1) * 128], krope[:, par, g * 128:(g + 1) * 128]) for g in range(2)],
               [("krope", par)], [tbk])
            op("scalar", "activation", [tbk, G1K], [("KT", t)], out=KT[:, :, t * 128:(t + 1) * 128],
               in_=tb[:, 0:256].rearrange("p (g t) -> p g t", g=2), func=AF.Copy)
            op("scalar", "activation", [bk, G1K, ("V", t)], [("V", t)], out=Vaug[:, t, :, 0:64],
               in_=bank[:, 256:384].rearrange("p (g d) -> p g d", g=2), func=AF.Copy)
        w_done()
        if stop == (l, "P2"):
            return True
        units = [(t, g) for t in range(NT) for g in range(2)]
        pend = []

        def stage_S(t, g, u):
            noprev = first_blk and t == 0
            kbs = [1] if noprev else [0, 1]
            banks = []
            for half in range(2):
                bank, bk = pf()
                banks.append((bank, bk))
                items = []
                reads = [("QT", t), G1K]
                for kb in kbs:
                    if kb == 1:
                        ksrc = KT[half * 64:(half + 1) * 64, g, t * 128:(t + 1) * 128]
                        reads.append(("KT", t))
                    elif t > 0:
                        ksrc = KT[half * 64:(half + 1) * 64, g, (t - 1) * 128:t * 128]
                        reads.append(("KT", t - 1))
                    else:
                        ksrc = KTs[half * 64:(half + 1) * 64, l, g, :]
                        reads.append(("KTs", l))
                    items.append((bank[:, kb * 256:(kb + 1) * 256].rearrange("p (c q) -> p c q", c=2), ksrc,
                                  QT[half * 64:(half + 1) * 64, 2 * g:2 * g + 2, t * 128:(t + 1) * 128], True, True))
                mm(items, reads, [bk])
            ue = u % 2
            for half in range(2):
                bank, bk = banks[half]
                lo = 256 if noprev else 0
                op("scalar", "activation", [bk], [("E", ue, half)], out=Ebuf[:, ue, half, lo:512], in_=bank[:, lo:512],
                   func=AF.Exp)
                if noprev:
                    ev = Ebuf[:, ue, half, 256:512].rearrange("p (c q) -> p c q", c=2)
                    mk = mask2b[:, 1, :].unsqueeze(1).broadcast_to([128, 2, 128])
                else:
                    ev = Ebuf[:, ue, half, :].rearrange("p (k c q) -> p k c q", k=2, c=2)
                    mk = mask2b[:, :, :].unsqueeze(2).broadcast_to([128, 2, 2, 128])
                op("vector", "tensor_tensor", [("E", ue, half), "mask2b"], [("E", ue, half)], out=ev, in0=ev, in1=mk,
                   op=ALU.mult)

        def stage_PV(t, g, u):
            noprev = first_blk and t == 0
            kbs = [1] if noprev else [0, 1]
            ue = u % 2
            bank, bk = pf()
            items = []
            reads = [("E", ue, 0), ("E", ue, 1), G1K]
            for half in range(2):
                for cc in range(2):
                    jj = half * 2 + cc
                    for i_, kb in enumerate(kbs):
                        if kb == 1:
                            vsrc = Vaug[:, t, g, :]
                            reads.append(("V", t))
                        elif t > 0:
                            vsrc = Vaug[:, t - 1, g, :]
                            reads.append(("V", t - 1))
                        else:
                            vsrc = Vs[:, l, g, :]
                            reads.append(("Vs", l))
                        items.append((bank[:, jj * 65:(jj + 1) * 65],
                                      Ebuf[:, ue, half, kb * 256 + cc * 128:kb * 256 + (cc + 1) * 128], vsrc,
                                      i_ == 0, i_ == len(kbs) - 1))
            mm(items, reads, [bk])
            ov = bank[:, 0:260].rearrange("p (h c e) -> p h c e", h=2, c=2)
            den, dk_ = sm(4)
            rden, rk_ = sm(4)
            op("vector", "tensor_tensor", [bk, "esink"], dk_, out=den.rearrange("p (h c) -> p h c", h=2),
               in0=ov[:, :, :, 64], in1=esink[:, l * 8 + g * 4:l * 8 + g * 4 + 4].rearrange("p (h c) -> p h c", h=2),
               op=ALU.add)
            op("vector", "reciprocal", dk_, rk_, out=rden, in_=den)
            ap_ = t % 2
            ao = Atok[:, ap_, g * 256:(g + 1) * 256].rearrange("p (c h d) -> p h c d", c=2, h=2)
            rb_ = rden.rearrange("p (h c) -> p h c", h=2).unsqueeze(3).broadcast_to([128, 2, 2, 64])
            op("vector", "tensor_tensor", [bk] + rk_, [("Atok", ap_, g)], out=ao, in0=ov[:, :, :, 0:64], in1=rb_,
               op=ALU.mult)

        def stage_T(t):
            ap_ = t % 2
            tb, tbk = pb()
            tr([(tb[:, c * 128:(c + 1) * 128], Atok[:, ap_, c * 128:(c + 1) * 128]) for c in range(4)],
               [("Atok", ap_, 0), ("Atok", ap_, 1)], [tbk])
            op("scalar", "activation", [tbk, G1K], [("AT", t)], out=AT[:, :, t * 128:(t + 1) * 128],
               in_=tb[:, 0:512].rearrange("p (c t) -> p c t", c=4), func=AF.Copy)

        LAG = 1
        for i in range(len(units) + LAG + 2):
            if i < len(units):
                stage_S(units[i][0], units[i][1], i)
            j = i - LAG
            if 0 <= j < len(units):
                stage_PV(units[j][0], units[j][1], j)
            k = i - LAG - 2
            if 0 <= k < len(units) and units[k][1] == 1:
                stage_T(units[k][0])
        for t in range(NT):
            pass
        op("vector", "tensor_copy", [("KT", 7), G1K], [("KTs", l)], out=KTs[:, l, :, :], in_=KT[:, :, 7 * 128:8 * 128])
        op("vector", "tensor_copy", [("V", 7), G1K], [("Vs", l)], out=Vs[:, l, :, :], in_=Vaug[:, 7, :, :])
        if stop == (l, "P3"):
            dump_big("AT", None, None) if False else None
            return True
        for nm, base in (("GA1", 0), ("GA2", 4)):
            wg, kg = w_get(nm, l)
            for fc in range(4):
                for half in range(2):
                    bank, bk = pf()
                    mm([(bank[:, :], wg[:, kc, fc * 128:(fc + 1) * 128], hT[:, kc, half * 512:(half + 1) * 512],
                         kc == 0, kc == 7) for kc in range(8)],
                       [("hT", tt_) for tt_ in range(half * 4, half * 4 + 4)] + [kg], [bk])
                    op("scalar", "activation", [bk], [("sgu", base + fc, half)],
                       out=sgu[:, base + fc, half * 512:(half + 1) * 512], in_=bank[:, :], func=SIGF)
            w_done()
        wa, ka = w_get("WA", l)
        for c in range(8):
            for half in range(2):
                bank, bk = pf()
                mm([(bank[:, :], wa[:, kc, c * 128:(c + 1) * 128], AT[:, kc, half * 512:(half + 1) * 512],
                     kc == 0, kc == 3) for kc in range(4)],
                   [("AT", tt_) for tt_ in range(half * 4, half * 4 + 4)] + [ka, G1K], [bk])
                op("vector", "tensor_tensor", [bk, ("sgu", c, half)], [("m1T", c, half)],
                   out=m1T[:, c, half * 512:(half + 1) * 512], in0=bank[:, :], in1=sgu[:, c, half * 512:(half + 1) * 512],
                   op=ALU.mult)
        w_done()
        if stop == (l, "P4"):
            return True
        barrier_g1()
        wts_h = {}
        pend_rT = []

        def A_ra(h, t):
            par = h % 2
            if t == 0:
                wts_h[h] = (w_get("RA%d" % h, l), w_get("RB%d" % h, l))
            (wra, kra), _ = wts_h[h]
            bank, bk = pf()
            mm([(bank[:, 0:512], hT[:, kc, t * 128:(t + 1) * 128], wra[:, kc, :], kc == 0, kc == 7) for kc in range(8)],
               [("hT", t), kra], [bk])
            qp = (h * 8 + t) % 2
            rope(bank[:, 0:128], 1, 64, tabs[:, t, 64:128], tabs[:, t, 128:192], cst[:, C_DQ + h:C_DQ + h + 1],
                 cst[:, C_NDQ + h:C_NDQ + h + 1], qtok[:, qp, :], [bk, "tabs", "cst"], [("qtok", qp)], [], 0)
            rope(bank[:, 128:256], 1, 64, tabs[:, t, 64:128], tabs[:, t, 128:192], cst[:, C_DK + h:C_DK + h + 1],
                 cst[:, C_NDK + h:C_NDK + h + 1], ktok[:, par, t, :], [bk, "tabs", "cst", G1K], [("ktok", par, t)], [], 128)
            op("scalar", "activation", [bk, G1K], [("vtok", par, t)], out=vtok[:, par, t, :], in_=bank[:, 256:512],
               func=AF.Copy)
            return qp

        def A_rb(h, t):
            par = h % 2
            _, (wrb, krb) = wts_h[h]
            bank, bk = pf()
            mm([(bank[:, 0:256], hT[:, kc, t * 128:(t + 1) * 128], wrb[:, kc, :], kc == 0, kc == 7) for kc in range(8)],
               [("hT", t), krb], [bk])
            op("scalar", "activation", [bk, G1K], [("gate", par, t)], out=gate[:, par, t, :], in_=bank[:, 0:256],
               func=AF.Silu)
            if t == NT - 1:
                w_done()
                w_done()

        def A_tr(h, t, qp):
            par = h % 2
            tb, tbk = pb()
            tr([(tb[:, 0:128], qtok[:, qp, :]), (tb[:, 128:256], ktok[:, par, t, :])],
               [("qtok", qp), ("ktok", par, t), G1K], [tbk])
            op("scalar", "activation", [tbk, G1K], [("qkT", par, t)], out=qkT[:, par, :, t * 128:(t + 1) * 128],
               in_=tb[:, 0:256].rearrange("p (a t) -> p a t", a=2), func=AF.Copy)

        def B_s_kv(h, t):
            par = h % 2
            first = first_blk and t == 0
            sp = (h * 8 + t) % 2
            bank, bk = pf()
            mm([(bank[:, 0:128], qkT[:, par, 1, t * 128:(t + 1) * 128], qkT[:, par, 0, t * 128:(t + 1) * 128], True, True)],
               [("qkT", par, t), G1K], [bk])
            op("vector", "tensor_tensor", [bk, "cst"], [("Sm", sp)], out=Smb[:, sp, :], in0=bank[:, 0:128],
               in1=cst[:, C_MLE:C_MLE + 128], op=ALU.mult)
            if (not first) and t == 0:
                op("scalar", "activation", [("Z", l, h)], [("Rb", 0)], out=Rb[:, 0, :], in_=Z[:, l, h, :], func=AF.Copy,
                   scale=GAM128[h])
            kvb, kvk = pf()
            mm([(kvb[:, 0:256], ktok[:, par, t, :], vtok[:, par, t, :], True, True)],
               [("ktok", par, t), ("vtok", par, t), G1K], [kvk])
            return sp, kvb, kvk

        def B_r(h, t, sp, kvb, kvk):
            par = h % 2
            first = first_blk and t == 0
            rbp = t % 2
            bank, bk = pf()
            items = [(bank[:, 0:256], Smb[:, sp, :], vtok[:, par, t, :], True, first)]
            reads = [("Sm", sp), ("vtok", par, t), G1K]
            if not first:
                items.append((bank[:, 0:256], qkT[:, par, 0, t * 128:(t + 1) * 128], Rb[:, rbp, :], False, True))
                reads += [("qkT", par, t), ("Rb", rbp)]
            mm(items, reads, [bk])
            op("vector", "scalar_tensor_tensor", [kvk, ("Z", l, h)], [("Z", l, h)], out=Z[:, l, h, :], in0=Z[:, l, h, :],
               scalar=GAM128[h], in1=kvb[:, 0:256], op0=ALU.mult, op1=ALU.add)
            if t < NT - 1:
                nb_ = (t + 1) % 2
                op("scalar", "activation", [("Z", l, h)], [("Rb", nb_)], out=Rb[:, nb_, :], in_=Z[:, l, h, :],
                   func=AF.Copy, scale=GAM128[h])
            ms, msk = sm(1)
            op("scalar", "activation", [bk], msk, out=junk[:, 0:256], in_=bank[:, 0:256], func=AF.Square,
               scale=1.0 / 16.0, accum_out=ms)
            r, rk = rstd_from_ms(ms, msk)
            rp = (h * 8 + t) % 2
            op("vector", "scalar_tensor_tensor", [bk, ("gate", par, t), G1K] + rk, [("rtok", rp)], out=rtok[:, rp, :],
               in0=bank[:, 0:256], scalar=r, in1=gate[:, par, t, :], op0=ALU.mult, op1=ALU.mult)
            return rp

        def B_tr(h, t, rp):
            tb, tbk = pb()
            tr([(tb[:, 0:128], rtok[:, rp, 0:128]), (tb[:, 128:256], rtok[:, rp, 128:256])], [("rtok", rp)], [tbk])
            op("scalar", "activation", [tbk], [("rT", h, t)], out=rT[:, 2 * h:2 * h + 2, t * 128:(t + 1) * 128],
               in_=tb[:, 0:256].rearrange("p (a t) -> p a t", a=2), func=AF.Copy)

        pend_btr = None
        for h in range(5):
            for t in range(NT):
                qp = None
                if h < 4:
                    qp = A_ra(h, t)
                bs = None
                if h >= 1:
                    bs = B_s_kv(h - 1, t)
                if pend_btr is not None:
                    B_tr(*pend_btr)
                    pend_btr = None
                if h < 4:
                    A_rb(h, t)
                if h >= 1:
                    rp = B_r(h - 1, t, *bs)
                    pend_btr = (h - 1, t, rp)
                if h < 4:
                    A_tr(h, t, qp)
        if pend_btr is not None:
            B_tr(*pend_btr)
        if stop == (l, "P5"):
            return True
        for nm, base in (("GB1", 0), ("GB2", 4)):
            wg, kg = w_get(nm, l)
            for fc in range(4):
                for half in range(2):
                    bank, bk = pf()
                    mm([(bank[:, :], wg[:, kc, fc * 128:(fc + 1) * 128], hT[:, kc, half * 512:(half + 1) * 512],
                         kc == 0, kc == 7) for kc in range(8)],
                       [("hT", tt_) for tt_ in range(half * 4, half * 4 + 4)] + [kg], [bk])
                    op("scalar", "activation", [bk], [("sgu", base + fc, half)],
                       out=sgu[:, base + fc, half * 512:(half + 1) * 512], in_=bank[:, :], func=SIGF)
            w_done()
        for nm, base in (("WB1", 0), ("WB2", 4)):
            wb_, kb_ = w_get(nm, l)
            for fc in range(4):
                c = base + fc
                for half in range(2):
                    bank, bk = pf()
                    mm([(bank[:, :], wb_[:, kc, fc * 128:(fc + 1) * 128], rT[:, kc, half * 512:(half + 1) * 512],
                         kc == 0, kc == 7) for kc in range(8)],
                       [("rT", hh, tt_) for hh in range(4) for tt_ in range(half * 4, half * 4 + 4)] + [kb_], [bk])
                    sp_ = st["scr"]
                    st["scr"] = 1 - sp_
                    skey = ("scr", sp_, 0)
                    op("vector", "tensor_tensor", [bk, ("sgu", c, half)], [skey, ("scr", sp_, 128)], out=scr[:, sp_, :],
                       in0=bank[:, :], in1=sgu[:, c, half * 512:(half + 1) * 512], op=ALU.mult)
                    op("vector", "tensor_tensor", [skey, ("m1T", c, half)], [("m1T", c, half)],
                       out=m1T[:, c, half * 512:(half + 1) * 512], in0=scr[:, sp_, :],
                       in1=m1T[:, c, half * 512:(half + 1) * 512], op=ALU.add)
            w_done()
        if stop == (l, "P6"):
            return True
        for nm, ch in (("WO1", 0), ("WO2", 1)):
            wo, ko = w_get(nm, l)
            for t in range(NT):
                bank, bk = pf()
                mm([(bank[:, :], m1T[:, kc, t * 128:(t + 1) * 128], wo[:, kc, :], kc == 0, kc == 7) for kc in range(8)],
                   [("m1T", kc, t // 4) for kc in range(8)] + [ko], [bk])
                op("vector", "tensor_tensor", [bk, ("x", t, ch)], [("x", t, ch)], out=xres[:, t, ch * 512:(ch + 1) * 512],
                   in0=bank[:, :], in1=xres[:, t, ch * 512:(ch + 1) * 512], op=ALU.add)
            w_done()
        if stop == (l, "P7"):
            return True
        norm_phase(l, C_GMLP + l * 8)
        for q in range(4):
            for nm, base in (("UP%da" % q, 0), ("UP%db" % q, 4)):
                wu, ku = w_get(nm, l)
                for fc in range(4):
                    for half in range(2):
                        bank, bk = pf()
                        mm([(bank[:, :], wu[:, kc, fc * 128:(fc + 1) * 128], hT[:, kc, half * 512:(half + 1) * 512],
                             kc == 0, kc == 7) for kc in range(8)],
                           [("hT", tt_) for tt_ in range(half * 4, half * 4 + 4)] + [ku], [bk])
                        sp_ = st["scr"]
                        st["scr"] = 1 - sp_
                        skey = ("scr", sp_, 0)
                        op("scalar", "activation", [bk], [skey, ("scr", sp_, 128)], out=scr[:, sp_, :], in_=bank[:, :],
                           func=AF.Copy)
                        op("vector", "scalar_tensor_tensor", [bk, skey], [("sgu", base + fc, half)],
                           out=sgu[:, base + fc, half * 512:(half + 1) * 512], in0=bank[:, :], scalar=0.0,
                           in1=scr[:, sp_, :], op0=ALU.max, op1=ALU.mult)
                w_done()
            for nm, ch in (("DN%da" % q, 0), ("DN%db" % q, 1)):
                wd, kd = w_get(nm, l)
                for t in range(NT):
                    bank, bk = pf()
                    mm([(bank[:, :], sgu[:, fc, t * 128:(t + 1) * 128], wd[:, fc, :], fc == 0, fc == 7) for fc in range(8)],
                       [("sgu", fc, t // 4) for fc in range(8)] + [kd], [bk])
                    op("vector", "tensor_tensor", [bk, ("x", t, ch)], [("x", t, ch)],
                       out=xres[:, t, ch * 512:(ch + 1) * 512], in0=bank[:, :], in1=xres[:, t, ch * 512:(ch + 1) * 512],
                       op=ALU.add)
                w_done()
        return False

    stopped = False
    for b in range(nblocks):
        P.epoch = b
        for t in range(NT):
            r0 = (b * NT + t) * 128
            P.dma("sync", lambda e, t=t, r0=r0: [e.dma_start(out=xres[:, t, :], in_=x_d[r0:r0 + 128, :])], "xl%d" % t, 1,
                  writes=xkeys(t))
        P.dma("sync", lambda e, b=b: [e.dma_start(out=tabs[:], in_=t_d[b * TBLK:(b + 1) * TBLK, :].rearrange(
            "(t p) c -> p t c", p=128))], "tbl", 1, writes=["tabs"])
        for l in range(nlayers):
            stopped = layer(b, l, l == nlayers - 1)
            if stopped:
                break
        if not stopped:
            for t in range(NT):
                ms, msk = sm(1)
                op("scalar", "activation", xkeys(t), msk, out=junk[:], in_=xres[:, t, :], func=AF.Square,
                   scale=1.0 / 32.0, accum_out=ms)
                r, rk = rstd_from_ms(ms, msk)
                op("vector", "scalar_tensor_tensor", xkeys(t) + rk + ["cst"], xkeys(t), out=xres[:, t, :],
                   in0=xres[:, t, :], scalar=r, in1=cst[:, C_GF:C_GF + 1024], op0=ALU.mult, op1=ALU.mult)
        for t in range(NT):
            r0 = (b * NT + t) * 128
            P.dma("sync", lambda e, t=t, r0=r0: [e.dma_start(out=o_d[r0:r0 + 128, :], in_=xres[:, t, :])], "xs%d" % t, 1,
                  reads=xkeys(t), writes=[("out", b, t)])
        if stopped:
            break
    for nm, tens in (("hT", hT), ("m1T", m1T), ("rT", rT), ("sgu", sgu)):
        if nm in dbg_d:
            for c in range(8):
                allk = list(P.last_writer.keys())
                keys = [k for k in allk if isinstance(k, tuple) and k[0] == nm]
                op("vector", "tensor_copy", keys + ["dstage"], ["dstage"], out=dstage[:, :], in_=tens[:, c, :])
                dd = dbg_d[nm]
                P.dma("sync", lambda e, c=c, dd=dd: [e.dma_start(out=dd[:, c * 1024:(c + 1) * 1024], in_=dstage[:, :])],
                      "dbgs", 1, reads=["dstage"], writes=[("dbg", nm, c)])
    if "g1" in dbg_d:
        for c in range(14):
            op("vector", "tensor_copy", [G1K, "dstage"], ["dstage", G1K], out=dstage[:, :], in_=g1[:, c * 1024:(c + 1) * 1024])
            dd = dbg_d["g1"]
            if c < 8:
                P.dma("sync", lambda e, c=c, dd=dd: [e.dma_start(out=dd[:, c * 1024:(c + 1) * 1024], in_=dstage[:, :])],
                      "dbgs", 1, reads=["dstage"], writes=[("dbg", "g1", c)])
    outk = [k for k in P.last_writer.keys() if isinstance(k, tuple) and k[0] in ("out", "dbg")]
    outk += [("w", s_) for s_ in range(4)] + ["tabs", "cst"]
    P.add("sync", lambda e: None, reads=outk, writes=[], name="final")

    sem_names = P.sem_names()
    sems = {n: es.enter_context(nc.semaphore(n)) for n in sem_names}
    P.emit(sems)
    es.close()
    return nc, P


_CACHE = {}


def kernel(x, g_mix, w_in, sinks, w_a, w_b, w_out, g_mlp, w_up, w_down, g_final):
    x = np.asarray(x, np.float32)
    B = x.shape[0]
    wts = prep_weights(np.asarray(w_in, np.float32), np.asarray(w_a, np.float32), np.asarray(w_b, np.float32),
                       np.asarray(w_out, np.float32), np.asarray(w_up, np.float32), np.asarray(w_down, np.float32))
    consts = prep_consts(np.asarray(g_mix, np.float32), np.asarray(sinks, np.float32), np.asarray(g_mlp, np.float32),
                         np.asarray(g_final, np.float32))
    tabs = prep_tabs(4096)
    nc, _ = build_program(4, 2)
    in_maps = [{"x": np.ascontiguousarray(x[b]), "wts": wts, "consts": consts, "tabs": tabs} for b in range(B)]
    res = run_bass_kernel_spmd(nc, in_maps, core_ids=list(range(B)))
    out = np.stack([np.asarray(r["out"], np.float32) for r in res.results], axis=0)
    return out
```

```python
import os
import numpy as np
from contextlib import ExitStack
import concourse.bass as bass
import concourse.mybir as mybir
from concourse.bass_utils import run_bass_kernel_spmd

F32 = mybir.dt.float32
BF16 = mybir.dt.bfloat16
AF = mybir.ActivationFunctionType
SIGF = getattr(AF, os.environ.get('KDBG_SIG', 'Sigmoid'))
ALU = mybir.AluOpType

D = 1024
NT = 8
TBLK = 1024
EPS = 1e-6
ENGINES = ("tensor", "scalar", "vector", "gpsimd", "sync")


class Op:
    __slots__ = ("eng", "fn", "reads", "writes", "dma_sem", "ndma", "idx", "deps",
                 "need_sig", "count", "sem", "name", "epoch")

    def __init__(self, eng, fn, reads, writes, dma_sem=None, ndma=0, name="", epoch=0):
        self.eng = eng
        self.fn = fn
        self.reads = reads
        self.writes = writes
        self.dma_sem = dma_sem
        self.ndma = ndma
        self.deps = []
        self.need_sig = False
        self.count = None
        self.sem = None
        self.name = name
        self.epoch = epoch


class Prog:
    def __init__(self, nc):
        self.nc = nc
        self.ops = []
        self.last_writer = {}
        self.readers = {}
        self.dma_counts = {}
        self.epoch = 0

    def add(self, eng, fn, reads=(), writes=(), name=""):
        reads = list(reads)
        writes = list(writes)
        for k in reads:
            if isinstance(k, tuple) and k[0] in ("pf", "pb") and k not in writes:
                writes.append(k)
        op = Op(eng, fn, reads, writes, name=name, epoch=self.epoch)
        self._track(op)
        return op

    def dma(self, eng, fn, sem_name, ndma, reads=(), writes=(), name=""):
        op = Op(eng, fn, list(reads), list(writes), dma_sem=sem_name, ndma=ndma, name=name,
                epoch=self.epoch)
        self._track(op)
        return op

    def _track(self, op):
        op.idx = len(self.ops)
        deps = {}
        for k in op.reads:
            w = self.last_writer.get(k)
            if w is not None:
                deps[w.idx] = w
        for k in op.writes:
            w = self.last_writer.get(k)
            if w is not None:
                deps[w.idx] = w
            for r in self.readers.get(k, ()):
                deps[r.idx] = r
        for k in op.reads:
            self.readers.setdefault(k, []).append(op)
        for k in op.writes:
            self.last_writer[k] = op
            self.readers[k] = []
        deps.pop(op.idx, None)
        op.deps = list(deps.values())
        self.ops.append(op)

    def sem_names(self):
        names = set()
        for op in self.ops:
            if op.dma_sem is not None:
                names.add(op.dma_sem)
            else:
                names.add("e_%s_%d" % (op.eng, op.epoch))
        return sorted(names)

    def emit(self, sems):
        nc = self.nc

        def skip(d, op):
            return (d.dma_sem is None and op.dma_sem is None and d.eng == "tensor" and op.eng == "tensor")

        for op in self.ops:
            for d in op.deps:
                if d.dma_sem is None and not skip(d, op):
                    d.need_sig = True
        counters = {}
        for op in self.ops:
            if op.dma_sem is not None:
                c = self.dma_counts.get(op.dma_sem, 0) + 16 * op.ndma
                self.dma_counts[op.dma_sem] = c
                op.count = c
                op.sem = op.dma_sem
            elif op.need_sig:
                s = "e_%s_%d" % (op.eng, op.epoch)
                counters[s] = counters.get(s, 0) + 1
                op.count = counters[s]
                op.sem = s
        self.max_counts = counters
        per_eng = {e: [] for e in ENGINES}
        for op in self.ops:
            per_eng[op.eng].append(op)
        stats = {e: 0 for e in ENGINES}

        def run_engine(ename, eng):
            waited = {}
            for op in per_eng[ename]:
                need = {}
                for d in op.deps:
                    if d.count is None or skip(d, op):
                        continue
                    if need.get(d.sem, 0) < d.count:
                        need[d.sem] = d.count
                for s, v in need.items():
                    if waited.get(s, 0) >= v:
                        continue
                    eng.wait_ge(sems[s], v)
                    waited[s] = v
                    stats[ename] += 1
                r = op.fn(eng)
                if op.dma_sem is not None:
                    assert len(r) == op.ndma, (op.name, len(r), op.ndma)
                    for ins in r:
                        ins.then_inc(sems[op.dma_sem], 16)
                elif op.need_sig:
                    assert r is not None, op.name
                    r.then_inc(sems[op.sem], 1)

        with nc.Block() as block:
            @block.tensor
            def _(e):
                run_engine("tensor", e)

            @block.scalar
            def _(e):
                run_engine("scalar", e)

            @block.vector
            def _(e):
                run_engine("vector", e)

            @block.gpsimd
            def _(e):
                run_engine("gpsimd", e)

            @block.sync
            def _(e):
                run_engine("sync", e)
        self.wait_stats = stats


ATT_Q, ATT_KV, RET_QK, RET_V = 512, 128, 512, 1024
OFF_AQ = 0
OFF_AK = 512
OFF_AV = 640
OFF_RQ = 768
OFF_RK = 1280
OFF_RV = 1792
OFF_RG = 2816
OFF_GA = 3840
OFF_GB = 4864

CHUNKS = [("A1", 8, 512), ("A2", 8, 384), ("GA1", 8, 512), ("GA2", 8, 512), ("WA", 4, 1024)]
for _h in range(4):
    CHUNKS += [("RA%d" % _h, 8, 512), ("RB%d" % _h, 8, 256)]
CHUNKS += [("GB1", 8, 512), ("GB2", 8, 512), ("WB1", 8, 512), ("WB2", 8, 512), ("WO1", 8, 512), ("WO2", 8, 512)]
for _q in range(4):
    CHUNKS += [("UP%da" % _q, 8, 512), ("UP%db" % _q, 8, 512), ("DN%da" % _q, 8, 512), ("DN%db" % _q, 8, 512)]
CH_SIZE = [kc * n for (_, kc, n) in CHUNKS]
CH_OFF = [int(v) for v in np.concatenate([[0], np.cumsum(CH_SIZE)[:-1]])]
CH_IDX = {name: i for i, (name, _, _) in enumerate(CHUNKS)}
WTOT = int(sum(CH_SIZE))
NCH = len(CHUNKS)

C_ID, C_MGT, C_MLE, C_GMIX, C_GMLP, C_SINK, C_DQ, C_NDQ, C_DK, C_NDK, C_GF = 0, 128, 256, 384, 400, 416, 432, 436, 440, 444, 448
NCONST = 448 + 1024
C_EPSQ = C_NDQ
NTAB = 192


def _chunk(W, cols, kc_n):
    sub = W[:, cols]
    return np.ascontiguousarray(sub.reshape(kc_n, 128, len(cols)).transpose(1, 0, 2)).reshape(128, -1)


def prep_weights(w_in, w_a, w_b, w_out, w_up, w_down):
    L = w_in.shape[0]
    out = np.empty((L, 128, WTOT), np.float32)
    perm = np.concatenate([np.arange(0, 128, 2), np.arange(1, 128, 2)])
    ar = np.arange
    for l in range(L):
        parts = {}
        parts["A1"] = _chunk(w_in[l], OFF_AQ + ar(512), 8)
        kcols = np.concatenate([OFF_AK + ar(64), OFF_AK + ar(64), OFF_AK + 64 + ar(64), OFF_AK + 64 + ar(64),
                                OFF_AV + ar(128)])
        parts["A2"] = _chunk(w_in[l], kcols, 8)
        parts["GA1"] = _chunk(w_in[l], OFF_GA + ar(512), 8)
        parts["GA2"] = _chunk(w_in[l], OFF_GA + 512 + ar(512), 8)
        parts["WA"] = _chunk(w_a[l], ar(1024), 4)
        for h in range(4):
            cols = np.concatenate([OFF_RQ + h * 128 + perm, OFF_RK + h * 128 + perm, OFF_RV + h * 256 + ar(256)])
            parts["RA%d" % h] = _chunk(w_in[l], cols, 8)
            parts["RB%d" % h] = _chunk(w_in[l], OFF_RG + h * 256 + ar(256), 8)
        parts["GB1"] = _chunk(w_in[l], OFF_GB + ar(512), 8)
        parts["GB2"] = _chunk(w_in[l], OFF_GB + 512 + ar(512), 8)
        parts["WB1"] = _chunk(w_b[l], ar(512), 8)
        parts["WB2"] = _chunk(w_b[l], 512 + ar(512), 8)
        parts["WO1"] = _chunk(w_out[l], ar(512), 8)
        parts["WO2"] = _chunk(w_out[l], 512 + ar(512), 8)
        for q in range(4):
            parts["UP%da" % q] = _chunk(w_up[l], q * 1024 + ar(512), 8)
            parts["UP%db" % q] = _chunk(w_up[l], q * 1024 + 512 + ar(512), 8)
            parts["DN%da" % q] = _chunk(w_down[l][q * 1024:(q + 1) * 1024], ar(512), 8)
            parts["DN%db" % q] = _chunk(w_down[l][q * 1024:(q + 1) * 1024], 512 + ar(512), 8)
        for i, (name, kc, n) in enumerate(CHUNKS):
            out[l, :, CH_OFF[i]:CH_OFF[i] + CH_SIZE[i]] = parts[name]
    return out


def prep_consts(g_mix, sinks, g_mlp, g_final):
    c = np.zeros((128, NCONST), np.float32)
    c[:, C_ID:C_ID + 128] = np.eye(128, dtype=np.float32)
    k = np.arange(128)[:, None]
    q = np.arange(128)[None, :]
    c[:, C_MGT:C_MGT + 128] = (k > q).astype(np.float32)
    c[:, C_MLE:C_MLE + 128] = (k <= q).astype(np.float32)
    L = g_mix.shape[0]
    for l in range(L):
        c[:, C_GMIX + l * 8:C_GMIX + l * 8 + 8] = g_mix[l].reshape(8, 128).T
        c[:, C_GMLP + l * 8:C_GMLP + l * 8 + 8] = g_mlp[l].reshape(8, 128).T
        for g in range(2):
            for half in range(2):
                for cc in range(2):
                    head = 4 * g + 2 * cc + half
                    c[:, C_SINK + l * 8 + g * 4 + half * 2 + cc] = sinks[l, head]
    i = np.arange(128, dtype=np.float64)
    for h in range(4):
        gam = 1.0 - 2.0 ** (-5.0 - h)
        dq = gam ** (i + 1.0)
        dk = gam ** (-(i + 1.0)) * (128.0 ** -0.5)
        c[:, C_DQ + h] = dq
        c[:, C_EPSQ + h] = EPS / (dq * dq)
        c[:, C_DK + h] = dk
        c[:, C_NDK + h] = -dk
    c[:, C_GF:C_GF + 1024] = g_final[None, :]
    return c


def prep_tabs(S=4096):
    pos = np.arange(S, dtype=np.float32)
    inv = (10000.0 ** (-np.arange(32, dtype=np.float32) / 32)).astype(np.float32)
    angA = pos[:, None] * inv[None, :]
    theta = (1.0 / (10000.0 ** np.linspace(0.0, 1.0, 64, dtype=np.float32))).astype(np.float32)
    angR = pos[:, None] * theta[None, :]
    t = np.concatenate([np.cos(angA), np.sin(angA), np.cos(angR), np.sin(angR)], axis=1)
    return np.ascontiguousarray(t.astype(np.float32))


GAM128 = [float((1.0 - 2.0 ** (-5.0 - h)) ** 128) for h in range(4)]


def build_program(nblocks=4, nlayers=2, stop=None, dumps=()):
    nc = bass.Bass("TRN2", target_bir_lowering=False)
    S = nblocks * TBLK
    x_d = nc.dram_tensor("x", [S, D], F32, kind="ExternalInput").ap()
    w_d = nc.dram_tensor("wts", [2, 128, WTOT], F32, kind="ExternalInput").ap()
    c_d = nc.dram_tensor("consts", [128, NCONST], F32, kind="ExternalInput").ap()
    t_d = nc.dram_tensor("tabs", [4096, NTAB], F32, kind="ExternalInput").ap()
    o_d = nc.dram_tensor("out", [S, D], F32, kind="ExternalOutput").ap()

    es = ExitStack()

    def sb(name, shape, dt):
        return es.enter_context(nc.sbuf_tensor("s_" + name, shape, dt))

    cst = sb("cst", [128, NCONST], F32)
    identb = sb("identb", [128, 128], BF16)
    mask2b = sb("mask2b", [128, 2, 128], BF16)
    esink = sb("esink", [128, 16], F32)
    mhalf = sb("mhalf", [128, 8], F32)
    dummy = sb("dummyk", [128, 8], F32)
    xres = sb("xres", [128, NT, D], F32)
    hT = sb("hT", [128, 8, TBLK], BF16)
    wsl = sb("wsl", [128, 4, 4096], BF16)
    tabs = sb("tabs", [128, NT, NTAB], F32)
    xn = sb("xn", [128, 2, D], BF16)
    junk = sb("junk", [128, D], BF16)
    scr = sb("scr", [128, 2, 512], F32)
    qrope = sb("qrope", [128, 2, 512], BF16)
    krope = sb("krope", [128, 2, 256], BF16)
    g1 = sb("g1", [128, 16384], BF16)
    KTs = sb("KTs", [128, 2, 2, 128], BF16)
    Vs = sb("Vs", [128, 2, 2, 65], BF16)
    Ebuf = sb("Ebuf", [128, 2, 2, 512], BF16)
    Atok = sb("Atok", [128, 2, 512], BF16)
    small = sb("small", [128, 64], F32)
    sgu = sb("sgu", [128, 8, TBLK], BF16)
    m1T = sb("m1T", [128, 8, TBLK], BF16)
    Smb = sb("Smb", [128, 2, 128], BF16)
    rtok = sb("rtok", [128, 2, 256], BF16)
    Z = sb("Z", [128, 2, 4, 256], F32)
    Rb = sb("Rb", [128, 2, 256], BF16)
    rT = sb("rT", [128, 8, TBLK], BF16)
    psf = es.enter_context(nc.psum_tensor("psf", [128, 6, 512], F32))
    psb = es.enter_context(nc.psum_tensor("psb", [128, 2, 1024], BF16))

    QT = g1[:, 0:4096].rearrange("p (c t) -> p c t", c=4)
    AT = g1[:, 4096:8192].rearrange("p (c t) -> p c t", c=4)
    KT = g1[:, 8192:10240].rearrange("p (g t) -> p g t", g=2)
    Vaug = g1[:, 10240:10240 + 8 * 2 * 65].rearrange("p (t g e) -> p t g e", t=8, g=2)
    qkT = g1[:, 0:4096].rearrange("p (a b t) -> p a b t", a=2, b=2)
    qktok = g1[:, 4096:8192].rearrange("p (a t d) -> p a t d", a=2, t=8)
    vtok = g1[:, 8192:12288].rearrange("p (a t d) -> p a t d", a=2, t=8)
    gate = g1[:, 12288:16384].rearrange("p (a t d) -> p a t d", a=2, t=8)
    G1K = ("g1",)

    P = Prog(nc)

    def op(eng, method, reads, writes, name="", **kw):
        return P.add(eng, lambda e: getattr(e, method)(**kw), reads, writes, name)

    def mm(items, reads, writes, name=""):
        def fn(e):
            ins = None
            for (o, l, r, st_, sp) in items:
                ins = e.matmul(o, l, r, start=st_, stop=sp)
            return ins
        return P.add("tensor", fn, reads, writes, name)

    def tr(items, reads, writes, name=""):
        def fn(e):
            ins = None
            for (o, i_) in items:
                ins = e.transpose(o, i_, identb[:])
            return ins
        return P.add("tensor", fn, reads + ["identb"], writes, name)

    st = {"pf": 0, "pb": 0, "sm": 0, "scr": 0}

    def pf():
        i = st["pf"]
        st["pf"] = (i + 1) % 6
        return psf[:, i, :], ("pf", i)

    def pb():
        i = st["pb"]
        st["pb"] = (i + 1) % 2
        return psb[:, i, :], ("pb", i)

    def sm(n):
        i = st["sm"]
        if i + n > 64:
            i = 0
        st["sm"] = i + n
        return small[:, i:i + n], [("sm", j) for j in range(i, i + n)]

    seq = [(l, ci) for b in range(nblocks) for l in range(nlayers) for ci in range(NCH)]
    wst = {"next_load": 0, "next_rel": 0, "done": set()}

    def w_issue():
        n = wst["next_load"]
        if n >= len(seq):
            return
        wst["next_load"] = n + 1
        l, ci = seq[n]
        slot = n % 4
        size = CH_SIZE[ci]
        src = w_d[l, :, CH_OFF[ci]:CH_OFF[ci] + size]
        dst = wsl[:, slot, 0:size]
        P.dma("gpsimd", lambda e: [e.dma_start(out=dst, in_=src, max_dma_last_dim=4096)], "w%d" % slot, 1,
              writes=[("w", slot)], name="wload%d" % n)

    def w_get(name, b, l):
        n = (b * nlayers + l) * NCH + CH_IDX[name]
        assert n < wst["next_load"], ("chunk not loaded yet (ring too small for this interleaving)", name, n)
        assert n >= wst["next_rel"]
        slot = n % 4
        kc, ncols = CHUNKS[CH_IDX[name]][1], CHUNKS[CH_IDX[name]][2]
        view = wsl[:, slot, 0:kc * ncols].rearrange("p (k n) -> p k n", k=kc)
        return view, ("w", slot), n

    def w_done(n):
        wst["done"].add(n)
        while wst["next_rel"] in wst["done"]:
            wst["done"].remove(wst["next_rel"])
            wst["next_rel"] += 1
            w_issue()

    P.dma("sync", lambda e: [e.dma_start(out=cst[:], in_=c_d[:, :])], "cstl", 1, writes=["cst"])
    op("vector", "tensor_copy", ["cst"], ["identb"], out=identb[:], in_=cst[:, C_ID:C_ID + 128])
    op("vector", "tensor_copy", ["cst"], ["mask2b"], out=mask2b[:, 0, :], in_=cst[:, C_MGT:C_MGT + 128])
    op("vector", "tensor_copy", ["cst", "mask2b"], ["mask2b"], out=mask2b[:, 1, :], in_=cst[:, C_MLE:C_MLE + 128])
    op("scalar", "activation", ["cst"], ["esink"], out=esink[:], in_=cst[:, C_SINK:C_SINK + 16], func=AF.Exp)
    op("vector", "memset", [], [("Z", l, h) for l in range(2) for h in range(4)], ap=Z[:], constant=0.0)
    op("gpsimd", "memset", [], ["mhalf"], ap=mhalf[:], constant=-0.5)
    for _ in range(4):
        w_issue()

    xkeys = lambda t: [("x", t, 0), ("x", t, 1)]

    def rstd_from_ms(ms_ap, ms_keys, eps_ap=None):
        v, vk = sm(1)
        r, rk = sm(1)
        if eps_ap is None:
            op("gpsimd", "tensor_scalar", ms_keys, vk, out=v, in0=ms_ap, scalar1=EPS, scalar2=None, op0=ALU.add)
        else:
            op("gpsimd", "tensor_tensor", ms_keys + ["cst"], vk, out=v, in0=ms_ap, in1=eps_ap, op=ALU.add)
        op("gpsimd", "tensor_tensor", vk + ["mhalf"], rk, out=r, in0=v, in1=mhalf[:, 0:1], op=ALU.pow)
        return r, rk

    def norm_pre(t):
        ms, msk = sm(1)
        op("scalar", "activation", xkeys(t), msk, out=junk[:], in_=xres[:, t, :], func=AF.Square,
           scale=1.0 / 32.0, accum_out=ms)
        r, rk = rstd_from_ms(ms, msk)
        par = t % 2
        op("scalar", "activation", xkeys(t) + rk, [("xn", par)], out=xn[:, par, :], in_=xres[:, t, :], func=AF.Copy,
           scale=r)

    def norm_post(t, gcol):
        par = t % 2
        bank, bk = pb()
        tr([(bank[:, kc * 128:(kc + 1) * 128], xn[:, par, kc * 128:(kc + 1) * 128]) for kc in range(8)],
           [("xn", par)], [bk])
        gb = cst[:, gcol:gcol + 8].unsqueeze(2).broadcast_to([128, 8, 128])
        op("vector", "tensor_tensor", [bk, "cst"], [("hT", t)], out=hT[:, :, t * 128:(t + 1) * 128],
           in0=bank.rearrange("p (k t) -> p k t", k=8), in1=gb, op=ALU.mult)

    def rope(xa, nh, half, cosb, sinb, sc, out_ap, rk, wk, scr_off):
        n = nh * 2 * half
        u = scr[:, 0, scr_off:scr_off + n]
        tt = scr[:, 1, scr_off:scr_off + n]
        xv = xa.rearrange("p (h two d) -> p h two d", two=2, d=half)
        uv = u.rearrange("p (h two d) -> p h two d", two=2, d=half)
        tv = tt.rearrange("p (h two d) -> p h two d", two=2, d=half)
        sbc = cosb.unsqueeze(1).broadcast_to([128, nh, half])
        sbb = sinb.unsqueeze(1).broadcast_to([128, nh, half])
        offs = [o for o in (0, 128) if (o == scr_off) or (scr_off < o < scr_off + n)]
        ku = [("scr", 0, o) for o in offs]
        kt_ = [("scr", 1, o) for o in offs]
        for j in range(2):
            op("vector", "scalar_tensor_tensor", rk + ku, ku, out=uv[:, :, j, :], in0=xv[:, :, j, :], scalar=sc, in1=sbc,
               op0=ALU.mult, op1=ALU.mult)
        op("vector", "scalar_tensor_tensor", rk + kt_, kt_, out=tv[:, :, 0, :], in0=xv[:, :, 1, :],
           scalar=-sc, in1=sbb, op0=ALU.mult, op1=ALU.mult)
        op("vector", "scalar_tensor_tensor", rk + kt_, kt_, out=tv[:, :, 1, :], in0=xv[:, :, 0, :],
           scalar=sc, in1=sbb, op0=ALU.mult, op1=ALU.mult)
        op("vector", "tensor_tensor", ku + kt_, wk, out=out_ap, in0=u, in1=tt, op=ALU.add)

    def barrier_g1():
        op("vector", "memset", [], [G1K], ap=dummy[:, 0:1], constant=0.0)

    def add_x(bank, bk, t, ch):
        op("vector", "tensor_tensor", [bk, ("x", t, ch)], [("x", t, ch)], out=xres[:, t, ch * 512:(ch + 1) * 512],
           in0=bank[:, :], in1=xres[:, t, ch * 512:(ch + 1) * 512], op=ALU.add)

    def gate_group(wg, kg, base, fc, half):
        bank, bk = pf()
        mm([(bank[:, :], wg[:, kc, fc * 128:(fc + 1) * 128], hT[:, kc, half * 512:(half + 1) * 512],
             kc == 0, kc == 7) for kc in range(8)],
           [("hT", tt_) for tt_ in range(half * 4, half * 4 + 4)] + [kg], [bk])
        op("scalar", "activation", [bk], [("sgu", base + fc, half)],
           out=sgu[:, base + fc, half * 512:(half + 1) * 512], in_=bank[:, :], func=AF.Sigmoid)

    def layer(b, l, norm_inline, tail_kind, next_gcol):
        first_blk = (b == 0)
        gcol1 = C_GMIX + l * 8
        barrier_g1()
        op("vector", "memset", [G1K], [("V", t) for t in range(NT)], ap=Vaug[:, :, :, 64:65], constant=1.0)
        wA1, kA1, nA1 = w_get("A1", b, l)
        pend = []
        if norm_inline:
            norm_pre(0)
            norm_pre(1)
        for t in range(NT):
            if norm_inline:
                norm_post(t, gcol1)
                if t + 2 < NT:
                    norm_pre(t + 2)
            bank, bk = pf()
            mm([(bank[:, 0:512], hT[:, kc, t * 128:(t + 1) * 128], wA1[:, kc, :], kc == 0, kc == 7) for kc in range(8)],
               [("hT", t), kA1], [bk])
            par = t % 2
            rope(bank[:, 0:512], 8, 32, tabs[:, t, 0:32], tabs[:, t, 32:64], 0.125, qrope[:, par, :],
                 [bk, "tabs"], [("qrope", par)], 0)

            def qtr(t=t, par=par):
                tb, tbk = pb()
                tr([(tb[:, c * 128:(c + 1) * 128], qrope[:, par, c * 128:(c + 1) * 128]) for c in range(4)],
                   [("qrope", par)], [tbk])
                op("scalar", "activation", [tbk, G1K], [("QT", t)], out=QT[:, :, t * 128:(t + 1) * 128],
                   in_=tb[:, 0:512].rearrange("p (c t) -> p c t", c=4), func=AF.Copy)
            if pend:
                pend.pop(0)()
            pend.append(qtr)
        w_done(nA1)
        wA2, kA2, nA2 = w_get("A2", b, l)
        for t in range(NT):
            bank, bk = pf()
            mm([(bank[:, 0:384], hT[:, kc, t * 128:(t + 1) * 128], wA2[:, kc, :], kc == 0, kc == 7) for kc in range(8)],
               [("hT", t), kA2], [bk])
            par = t % 2
            rope(bank[:, 0:256], 4, 32, tabs[:, t, 0:32], tabs[:, t, 32:64], 1.0, krope[:, par, :],
                 [bk, "tabs"], [("krope", par)], 0)
            op("scalar", "activation", [bk, G1K, ("V", t)], [("V", t)], out=Vaug[:, t, :, 0:64],
               in_=bank[:, 256:384].rearrange("p (g d) -> p g d", g=2), func=AF.Copy)

            def ktr(t=t, par=par):
                tb, tbk = pb()
                tr([(tb[:, g * 128:(g + 1) * 128], krope[:, par, g * 128:(g + 1) * 128]) for g in range(2)],
                   [("krope", par)], [tbk])
                op("scalar", "activation", [tbk, G1K], [("KT", t)], out=KT[:, :, t * 128:(t + 1) * 128],
                   in_=tb[:, 0:256].rearrange("p (g t) -> p g t", g=2), func=AF.Copy)
            if pend:
                pend.pop(0)()
            pend.append(ktr)
        w_done(nA2)
        while pend:
            pend.pop(0)()
        units = [(t, g) for t in range(NT) for g in range(2)]

        def stage_S(t, g, u):
            noprev = first_blk and t == 0
            kbs = [1] if noprev else [0, 1]
            banks = []
            for half in range(2):
                bank, bk = pf()
                banks.append((bank, bk))
                items = []
                reads = [("QT", t), G1K]
                for kb in kbs:
                    if kb == 1:
                        ksrc = KT[half * 64:(half + 1) * 64, g, t * 128:(t + 1) * 128]
                        reads.append(("KT", t))
                    elif t > 0:
                        ksrc = KT[half * 64:(half + 1) * 64, g, (t - 1) * 128:t * 128]
                        reads.append(("KT", t - 1))
                    else:
                        ksrc = KTs[half * 64:(half + 1) * 64, l, g, :]
                        reads.append(("KTs", l))
                    items.append((bank[:, kb * 256:(kb + 1) * 256].rearrange("p (c q) -> p c q", c=2), ksrc,
                                  QT[half * 64:(half + 1) * 64, 2 * g:2 * g + 2, t * 128:(t + 1) * 128], True, True))
                mm(items, reads, [bk])
            ue = u % 2
            for half in range(2):
                bank, bk = banks[half]
                lo = 256 if noprev else 0
                op("scalar", "activation", [bk], [("E", ue, half)], out=Ebuf[:, ue, half, lo:512], in_=bank[:, lo:512],
                   func=AF.Exp)
                if noprev:
                    ev = Ebuf[:, ue, half, 256:512].rearrange("p (c q) -> p c q", c=2)
                    mk = mask2b[:, 1, :].unsqueeze(1).broadcast_to([128, 2, 128])
                else:
                    ev = Ebuf[:, ue, half, :].rearrange("p (k c q) -> p k c q", k=2, c=2)
                    mk = mask2b[:, :, :].unsqueeze(2).broadcast_to([128, 2, 2, 128])
                op("vector", "tensor_tensor", [("E", ue, half), "mask2b"], [("E", ue, half)], out=ev, in0=ev, in1=mk,
                   op=ALU.mult)

        def stage_PV(t, g, u):
            noprev = first_blk and t == 0
            kbs = [1] if noprev else [0, 1]
            ue = u % 2
            bank, bk = pf()
            items = []
            reads = [("E", ue, 0), ("E", ue, 1), G1K]
            for half in range(2):
                for cc in range(2):
                    jj = half * 2 + cc
                    for i_, kb in enumerate(kbs):
                        if kb == 1:
                            vsrc = Vaug[:, t, g, :]
                            reads.append(("V", t))
                        elif t > 0:
                            vsrc = Vaug[:, t - 1, g, :]
                            reads.append(("V", t - 1))
                        else:
                            vsrc = Vs[:, l, g, :]
                            reads.append(("Vs", l))
                        items.append((bank[:, jj * 65:(jj + 1) * 65],
                                      Ebuf[:, ue, half, kb * 256 + cc * 128:kb * 256 + (cc + 1) * 128], vsrc,
                                      i_ == 0, i_ == len(kbs) - 1))
            mm(items, reads, [bk])
            ov = bank[:, 0:260].rearrange("p (h c e) -> p h c e", h=2, c=2)
            den, dk_ = sm(4)
            rden, rk_ = sm(4)
            op("vector", "tensor_tensor", [bk, "esink"], dk_, out=den.rearrange("p (h c) -> p h c", h=2),
               in0=ov[:, :, :, 64], in1=esink[:, l * 8 + g * 4:l * 8 + g * 4 + 4].rearrange("p (h c) -> p h c", h=2),
               op=ALU.add)
            op("vector", "reciprocal", dk_, rk_, out=rden, in_=den)
            ap_ = t % 2
            ao = Atok[:, ap_, g * 256:(g + 1) * 256].rearrange("p (c h d) -> p h c d", c=2, h=2)
            rb_ = rden.rearrange("p (h c) -> p h c", h=2).unsqueeze(3).broadcast_to([128, 2, 2, 64])
            op("vector", "tensor_tensor", [bk] + rk_, [("Atok", ap_, g)], out=ao, in0=ov[:, :, :, 0:64], in1=rb_,
               op=ALU.mult)

        def stage_T(t):
            ap_ = t % 2
            tb, tbk = pb()
            tr([(tb[:, c * 128:(c + 1) * 128], Atok[:, ap_, c * 128:(c + 1) * 128]) for c in range(4)],
               [("Atok", ap_, 0), ("Atok", ap_, 1)], [tbk])
            op("scalar", "activation", [tbk, G1K], [("AT", t)], out=AT[:, :, t * 128:(t + 1) * 128],
               in_=tb[:, 0:512].rearrange("p (c t) -> p c t", c=4), func=AF.Copy)

        ga_groups = [(nm, base, fc, half) for (nm, base) in (("GA1", 0), ("GA2", 4)) for fc in range(4) for half in range(2)]
        ga_w = {}
        for i in range(len(units) + 3):
            if i < len(units):
                stage_S(units[i][0], units[i][1], i)
            if i < len(ga_groups):
                nm, base, fc, half = ga_groups[i]
                if nm not in ga_w:
                    ga_w[nm] = w_get(nm, b, l)
                wg, kg, ng = ga_w[nm]
                gate_group(wg, kg, base, fc, half)
                if fc == 3 and half == 1:
                    w_done(ng)
            j = i - 1
            if 0 <= j < len(units):
                stage_PV(units[j][0], units[j][1], j)
            k = i - 3
            if 0 <= k < len(units) and units[k][1] == 1:
                stage_T(units[k][0])
        op("vector", "tensor_copy", [("KT", 7), G1K], [("KTs", l)], out=KTs[:, l, :, :], in_=KT[:, :, 7 * 128:8 * 128])
        op("vector", "tensor_copy", [("V", 7), G1K], [("Vs", l)], out=Vs[:, l, :, :], in_=Vaug[:, 7, :, :])
        wa, ka, na = w_get("WA", b, l)
        for half in range(2):
            for c in range(8):
                bank, bk = pf()
                mm([(bank[:, :], wa[:, kc, c * 128:(c + 1) * 128], AT[:, kc, half * 512:(half + 1) * 512],
                     kc == 0, kc == 3) for kc in range(4)],
                   [("AT", tt_) for tt_ in range(half * 4, half * 4 + 4)] + [ka, G1K], [bk])
                op("vector", "tensor_tensor", [bk, ("sgu", c, half)], [("m1T", c, half)],
                   out=m1T[:, c, half * 512:(half + 1) * 512], in0=bank[:, :], in1=sgu[:, c, half * 512:(half + 1) * 512],
                   op=ALU.mult)
        w_done(na)
        barrier_g1()
        wts_h = {}

        def A_ra(h, t):
            par = h % 2
            if t == 0:
                wts_h[h] = (w_get("RA%d" % h, b, l), w_get("RB%d" % h, b, l))
            (wra, kra, _), _ = wts_h[h]
            bank, bk = pf()
            mm([(bank[:, 0:512], hT[:, kc, t * 128:(t + 1) * 128], wra[:, kc, :], kc == 0, kc == 7) for kc in range(8)],
               [("hT", t), kra], [bk])
            return bank, bk

        def A_rope(h, t, bank, bk):
            par = h % 2
            rope(bank[:, 0:256], 2, 64, tabs[:, t, 64:128], tabs[:, t, 128:192], 1.0, qktok[:, par, t, :],
                 [bk, "tabs", G1K], [("qktok", par, t)], 128)
            op("scalar", "activation", [bk, G1K, "cst"], [("vtok", par, t)], out=vtok[:, par, t, :], in_=bank[:, 256:512],
               func=AF.Copy, scale=cst[:, C_DK + h:C_DK + h + 1])

        def A_rb(h, t):
            par = h % 2
            _, (wrb, krb, _) = wts_h[h]
            bank, bk = pf()
            mm([(bank[:, 0:256], hT[:, kc, t * 128:(t + 1) * 128], wrb[:, kc, :], kc == 0, kc == 7) for kc in range(8)],
               [("hT", t), krb], [bk])
            op("scalar", "activation", [bk, G1K], [("gate", par, t)], out=gate[:, par, t, :], in_=bank[:, 0:256],
               func=AF.Silu)
            if t == NT - 1:
                w_done(wts_h[h][0][2])
                w_done(wts_h[h][1][2])

        def A_tr(h, t):
            par = h % 2
            tb, tbk = pb()
            tr([(tb[:, 0:128], qktok[:, par, t, 0:128]), (tb[:, 128:256], qktok[:, par, t, 128:256])],
               [("qktok", par, t), G1K], [tbk])
            op("scalar", "activation", [tbk, G1K], [("qkT", par, t)], out=qkT[:, par, :, t * 128:(t + 1) * 128],
               in_=tb[:, 0:256].rearrange("p (a t) -> p a t", a=2), func=AF.Copy)

        def B_s_kv(h, t):
            par = h % 2
            first = first_blk and t == 0
            sp = (h * 8 + t) % 2
            bank, bk = pf()
            mm([(bank[:, 0:128], qkT[:, par, 1, t * 128:(t + 1) * 128], qkT[:, par, 0, t * 128:(t + 1) * 128], True, True)],
               [("qkT", par, t), G1K], [bk])
            kvb, kvk = pf()
            mm([(kvb[:, 0:256], qktok[:, par, t, 128:256], vtok[:, par, t, :], True, True)],
               [("qktok", par, t), ("vtok", par, t), G1K], [kvk])
            op("vector", "tensor_tensor", [bk, "cst"], [("Sm", sp)], out=Smb[:, sp, :], in0=bank[:, 0:128],
               in1=cst[:, C_MLE:C_MLE + 128], op=ALU.mult)
            if (not first) and t == 0:
                op("scalar", "activation", [("Z", l, h)], [("Rb", 0)], out=Rb[:, 0, :], in_=Z[:, l, h, :], func=AF.Copy,
                   scale=GAM128[h])
            return sp, kvb, kvk

        def B_r(h, t, sp, kvb, kvk):
            par = h % 2
            first = first_blk and t == 0
            rbp = t % 2
            bank, bk = pf()
            items = [(bank[:, 0:256], Smb[:, sp, :], vtok[:, par, t, :], True, first)]
            reads = [("Sm", sp), ("vtok", par, t), G1K]
            if not first:
                items.append((bank[:, 0:256], qkT[:, par, 0, t * 128:(t + 1) * 128], Rb[:, rbp, :], False, True))
                reads += [("qkT", par, t), ("Rb", rbp)]
            mm(items, reads, [bk])
            op("vector", "scalar_tensor_tensor", [kvk, ("Z", l, h)], [("Z", l, h)], out=Z[:, l, h, :], in0=Z[:, l, h, :],
               scalar=GAM128[h], in1=kvb[:, 0:256], op0=ALU.mult, op1=ALU.add)
            if t < NT - 1:
                nb_ = (t + 1) % 2
                op("scalar", "activation", [("Z", l, h)], [("Rb", nb_)], out=Rb[:, nb_, :], in_=Z[:, l, h, :],
                   func=AF.Copy, scale=GAM128[h])
            ms, msk = sm(1)
            op("scalar", "activation", [bk], msk, out=junk[:, 0:256], in_=bank[:, 0:256], func=AF.Square,
               scale=1.0 / 16.0, accum_out=ms)
            r, rk = rstd_from_ms(ms, msk, eps_ap=cst[:, C_EPSQ + h:C_EPSQ + h + 1])
            rp = (h * 8 + t) % 2
            op("vector", "scalar_tensor_tensor", [bk, ("gate", par, t), G1K] + rk, [("rtok", rp)], out=rtok[:, rp, :],
               in0=bank[:, 0:256], scalar=r, in1=gate[:, par, t, :], op0=ALU.mult, op1=ALU.mult)
            return rp

        def B_tr(h, t, rp):
            tb, tbk = pb()
            tr([(tb[:, 0:128], rtok[:, rp, 0:128]), (tb[:, 128:256], rtok[:, rp, 128:256])], [("rtok", rp)], [tbk])
            op("scalar", "activation", [tbk], [("rT", h, t)], out=rT[:, 2 * h:2 * h + 2, t * 128:(t + 1) * 128],
               in_=tb[:, 0:256].rearrange("p (a t) -> p a t", a=2), func=AF.Copy)

        gb_groups = [(nm, base, fc, half) for (nm, base) in (("GB1", 0), ("GB2", 4)) for fc in range(4) for half in range(2)]
        gb_w = {}

        def gb_fill(i):
            nm, base, fc, half = gb_groups[i]
            if nm not in gb_w:
                gb_w[nm] = w_get(nm, b, l)
            wg, kg, ng = gb_w[nm]
            gate_group(wg, kg, base, fc, half)
            if fc == 3 and half == 1:
                w_done(ng)

        pend_btr = None
        pend_atr = None
        for h in range(5):
            for t in range(NT):
                ab = None
                if h < 4:
                    ab = A_ra(h, t)
                else:
                    gb_fill(2 * t)
                bs = None
                if h >= 1:
                    bs = B_s_kv(h - 1, t)
                if pend_btr is not None:
                    B_tr(*pend_btr)
                    pend_btr = None
                if h < 4:
                    A_rb(h, t)
                    A_rope(h, t, *ab)
                else:
                    gb_fill(2 * t + 1)
                if h >= 1:
                    rp = B_r(h - 1, t, *bs)
                    pend_btr = (h - 1, t, rp)
                if pend_atr is not None:
                    A_tr(*pend_atr)
                    pend_atr = None
                if h < 4:
                    pend_atr = (h, t)
                    if t == NT - 1:
                        A_tr(*pend_atr)
                        pend_atr = None
        for nm, base in (("WB1", 0), ("WB2", 4)):
            wb_, kb_, nb_w = w_get(nm, b, l)
            for half in range(2):
                for fc in range(4):
                    c = base + fc
                    bank, bk = pf()
                    mm([(bank[:, :], wb_[:, kc, fc * 128:(fc + 1) * 128], rT[:, kc, half * 512:(half + 1) * 512],
                         kc == 0, kc == 7) for kc in range(8)],
                       [("rT", hh, tt_) for hh in range(4) for tt_ in range(half * 4, half * 4 + 4)] + [kb_], [bk])
                    if pend_btr is not None:
                        B_tr(*pend_btr)
                        pend_btr = None
                    sp_ = st["scr"]
                    st["scr"] = 1 - sp_
                    skey = ("scr", sp_, 0)
                    op("vector", "tensor_tensor", [bk, ("sgu", c, half)], [skey, ("scr", sp_, 128)], out=scr[:, sp_, :],
                       in0=bank[:, :], in1=sgu[:, c, half * 512:(half + 1) * 512], op=ALU.mult)
                    op("vector", "tensor_tensor", [skey, ("m1T", c, half)], [("m1T", c, half)],
                       out=m1T[:, c, half * 512:(half + 1) * 512], in0=scr[:, sp_, :],
                       in1=m1T[:, c, half * 512:(half + 1) * 512], op=ALU.add)
            w_done(nb_w)
        gcol2 = C_GMLP + l * 8
        wo1, ko1, no1 = w_get("WO1", b, l)
        wo2, ko2, no2 = w_get("WO2", b, l)
        pend_post = []
        for t in range(NT):
            for ch, (wo, ko) in enumerate(((wo1, ko1), (wo2, ko2))):
                bank, bk = pf()
                mm([(bank[:, :], m1T[:, kc, t * 128:(t + 1) * 128], wo[:, kc, :], kc == 0, kc == 7) for kc in range(8)],
                   [("m1T", kc, t // 4) for kc in range(8)] + [ko], [bk])
                add_x(bank, bk, t, ch)
            norm_pre(t)
            if pend_post:
                norm_post(pend_post.pop(0), gcol2)
            pend_post.append(t)
        w_done(no1)
        w_done(no2)
        for q in range(4):
            gi = 0
            for nm, base in (("UP%da" % q, 0), ("UP%db" % q, 4)):
                wu, ku, nu = w_get(nm, b, l)
                for half in range(2):
                    for fc in range(4):
                        bank, bk = pf()
                        mm([(bank[:, :], wu[:, kc, fc * 128:(fc + 1) * 128], hT[:, kc, half * 512:(half + 1) * 512],
                             kc == 0, kc == 7) for kc in range(8)],
                           [("hT", tt_) for tt_ in range(half * 4, half * 4 + 4)] + [ku], [bk])
                        gi += 1
                        if pend_post and gi == 2:
                            norm_post(pend_post.pop(0), gcol2)
                        sp_ = st["scr"]
                        st["scr"] = 1 - sp_
                        skey = ("scr", sp_, 0)
                        op("scalar", "activation", [bk], [skey, ("scr", sp_, 128)], out=scr[:, sp_, :], in_=bank[:, :],
                           func=AF.Copy)
                        op("vector", "scalar_tensor_tensor", [bk, skey], [("sgu", base + fc, half)],
                           out=sgu[:, base + fc, half * 512:(half + 1) * 512], in0=bank[:, :], scalar=0.0,
                           in1=scr[:, sp_, :], op0=ALU.max, op1=ALU.mult)
                w_done(nu)
            if q < 3:
                for nm, ch in (("DN%da" % q, 0), ("DN%db" % q, 1)):
                    wd, kd, nd = w_get(nm, b, l)
                    for t in range(NT):
                        bank, bk = pf()
                        mm([(bank[:, :], sgu[:, fc, t * 128:(t + 1) * 128], wd[:, fc, :], fc == 0, fc == 7)
                            for fc in range(8)], [("sgu", fc, t // 4) for fc in range(8)] + [kd], [bk])
                        add_x(bank, bk, t, ch)
                    w_done(nd)
            else:
                wd1, kd1, nd1 = w_get("DN3a", b, l)
                wd2, kd2, nd2 = w_get("DN3b", b, l)
                pend_n = []
                for t in range(NT):
                    for ch, (wd, kd) in enumerate(((wd1, kd1), (wd2, kd2))):
                        bank, bk = pf()
                        mm([(bank[:, :], sgu[:, fc, t * 128:(t + 1) * 128], wd[:, fc, :], fc == 0, fc == 7)
                            for fc in range(8)], [("sgu", fc, t // 4) for fc in range(8)] + [kd], [bk])
                        add_x(bank, bk, t, ch)
                    if tail_kind == "norm":
                        norm_pre(t)
                        if pend_n:
                            norm_post(pend_n.pop(0), next_gcol)
                        pend_n.append(t)
                    else:
                        ms, msk = sm(1)
                        op("scalar", "activation", xkeys(t), msk, out=junk[:], in_=xres[:, t, :], func=AF.Square,
                           scale=1.0 / 32.0, accum_out=ms)
                        r, rk = rstd_from_ms(ms, msk)
                        op("vector", "scalar_tensor_tensor", xkeys(t) + rk + ["cst"], xkeys(t), out=xres[:, t, :],
                           in0=xres[:, t, :], scalar=r, in1=cst[:, C_GF:C_GF + 1024], op0=ALU.mult, op1=ALU.mult)
                        r0 = (b * NT + t) * 128
                        P.dma("sync", lambda e, t=t, r0=r0: [e.dma_start(out=o_d[r0:r0 + 128, :], in_=xres[:, t, :])],
                              "xs%d" % t, 1, reads=xkeys(t), writes=[("out", b, t)])
                w_done(nd1)
                w_done(nd2)
                while pend_n:
                    norm_post(pend_n.pop(0), next_gcol)

    for b in range(nblocks):
        P.epoch = b
        for t in range(NT):
            r0 = (b * NT + t) * 128
            P.dma("sync", lambda e, t=t, r0=r0: [e.dma_start(out=xres[:, t, :], in_=x_d[r0:r0 + 128, :])], "xl%d" % t, 1,
                  writes=xkeys(t))
        P.dma("sync", lambda e, b=b: [e.dma_start(out=tabs[:], in_=t_d[b * TBLK:(b + 1) * TBLK, :].rearrange(
            "(t p) c -> p t c", p=128))], "tbl", 1, writes=["tabs"])
        for l in range(nlayers):
            last = (l == nlayers - 1)
            layer(b, l, norm_inline=(l == 0), tail_kind=("final" if last else "norm"),
                  next_gcol=(None if last else C_GMIX + (l + 1) * 8))
    outk = [k for k in P.last_writer.keys() if isinstance(k, tuple) and k[0] in ("out",)]
    outk += [("w", s_) for s_ in range(4)] + ["tabs", "cst"]
    P.add("sync", lambda e: None, reads=outk, writes=[], name="final")

    sem_names = P.sem_names()
    sems = {n: es.enter_context(nc.semaphore(n)) for n in sem_names}
    P.emit(sems)
    es.close()
    return nc, P


_CACHE = {}


def kernel(x, g_mix, w_in, sinks, w_a, w_b, w_out, g_mlp, w_up, w_down, g_final):
    x = np.asarray(x, np.float32)
    B = x.shape[0]
    wts = prep_weights(np.asarray(w_in, np.float32), np.asarray(w_a, np.float32), np.asarray(w_b, np.float32),
                       np.asarray(w_out, np.float32), np.asarray(w_up, np.float32), np.asarray(w_down, np.float32))
    consts = prep_consts(np.asarray(g_mix, np.float32), np.asarray(sinks, np.float32), np.asarray(g_mlp, np.float32),
                         np.asarray(g_final, np.float32))
    tabs = prep_tabs(4096)
    nc, _ = build_program(4, 2)
    in_maps = [{"x": np.ascontiguousarray(x[b]), "wts": wts, "consts": consts, "tabs": tabs} for b in range(B)]
    res = run_bass_kernel_spmd(nc, in_maps, core_ids=list(range(B)))
    out = np.stack([np.asarray(r["out"], np.float32) for r in res.results], axis=0)
    return out
```

```python
import os
import numpy as np
from contextlib import ExitStack
import concourse.bass as bass
import concourse.mybir as mybir
from concourse.bass_utils import run_bass_kernel_spmd

F32 = mybir.dt.float32
BF16 = mybir.dt.bfloat16
AF = mybir.ActivationFunctionType
SIGF = getattr(AF, os.environ.get('KDBG_SIG', 'Sigmoid'))
ALU = mybir.AluOpType

D = 1024
NT = 8
TBLK = 1024
EPS = 1e-6
ENGINES = ("tensor", "scalar", "vector", "gpsimd", "sync")


class Op:
    __slots__ = ("eng", "fn", "reads", "writes", "dma_sem", "ndma", "idx", "deps",
                 "need_sig", "count", "sem", "name", "epoch", "true_writes", "raw")

    def __init__(self, eng, fn, reads, writes, dma_sem=None, ndma=0, name="", epoch=0):
        self.eng = eng
        self.fn = fn
        self.reads = reads
        self.writes = writes
        self.dma_sem = dma_sem
        self.ndma = ndma
        self.deps = []
        self.need_sig = False
        self.count = None
        self.sem = None
        self.name = name
        self.epoch = epoch


class Prog:
    def __init__(self, nc):
        self.nc = nc
        self.ops = []
        self.last_writer = {}
        self.readers = {}
        self.dma_counts = {}
        self.epoch = 0

    def add(self, eng, fn, reads=(), writes=(), name=""):
        reads = list(reads)
        writes = list(writes)
        op_tw = set(writes)
        for k in reads:
            if isinstance(k, tuple) and k[0] in ("pf", "pb") and k not in writes:
                writes.append(k)
        op = Op(eng, fn, reads, writes, name=name, epoch=self.epoch)
        op.true_writes = op_tw
        self._track(op)
        return op

    def dma(self, eng, fn, sem_name, ndma, reads=(), writes=(), name=""):
        op = Op(eng, fn, list(reads), list(writes), dma_sem=sem_name, ndma=ndma, name=name,
                epoch=self.epoch)
        op.true_writes = set(op.writes)
        self._track(op)
        return op

    def _track(self, op):
        op.idx = len(self.ops)
        deps = {}
        for k in op.reads:
            w = self.last_writer.get(k)
            if w is not None:
                deps[w.idx] = w
        for k in op.writes:
            w = self.last_writer.get(k)
            if w is not None:
                deps[w.idx] = w
            for r in self.readers.get(k, ()):
                deps[r.idx] = r
        for k in op.reads:
            self.readers.setdefault(k, []).append(op)
        for k in op.writes:
            self.last_writer[k] = op
            self.readers[k] = []
        deps.pop(op.idx, None)
        op.deps = list(deps.values())
        rset = set(op.reads)
        op.raw = set(d.idx for d in op.deps if d.true_writes & rset)
        self.ops.append(op)

    def sem_names(self):
        names = set()
        for op in self.ops:
            if op.dma_sem is not None:
                names.add(op.dma_sem)
            else:
                names.add("e_%s_%d" % (op.eng, op.epoch))
        return sorted(names)

    def emit(self, sems):
        nc = self.nc

        def skip(d, op):
            if d.dma_sem is not None or op.dma_sem is not None or d.eng != op.eng:
                return False
            if d.eng == "tensor":
                return True
            return d.idx not in op.raw

        for op in self.ops:
            for d in op.deps:
                if d.dma_sem is None and not skip(d, op):
                    d.need_sig = True
        counters = {}
        for op in self.ops:
            if op.dma_sem is not None:
                c = self.dma_counts.get(op.dma_sem, 0) + 16 * op.ndma
                self.dma_counts[op.dma_sem] = c
                op.count = c
                op.sem = op.dma_sem
            elif op.need_sig:
                s = "e_%s_%d" % (op.eng, op.epoch)
                counters[s] = counters.get(s, 0) + 1
                op.count = counters[s]
                op.sem = s
        self.max_counts = counters
        per_eng = {e: [] for e in ENGINES}
        for op in self.ops:
            per_eng[op.eng].append(op)
        stats = {e: 0 for e in ENGINES}

        def run_engine(ename, eng):
            waited = {}
            for op in per_eng[ename]:
                need = {}
                for d in op.deps:
                    if d.count is None or skip(d, op):
                        continue
                    if need.get(d.sem, 0) < d.count:
                        need[d.sem] = d.count
                for s, v in need.items():
                    if waited.get(s, 0) >= v:
                        continue
                    eng.wait_ge(sems[s], v)
                    waited[s] = v
                    stats[ename] += 1
                r = op.fn(eng)
                if op.dma_sem is not None:
                    assert len(r) == op.ndma, (op.name, len(r), op.ndma)
                    for ins in r:
                        ins.then_inc(sems[op.dma_sem], 16)
                elif op.need_sig:
                    assert r is not None, op.name
                    r.then_inc(sems[op.sem], 1)

        with nc.Block() as block:
            @block.tensor
            def _(e):
                run_engine("tensor", e)

            @block.scalar
            def _(e):
                run_engine("scalar", e)

            @block.vector
            def _(e):
                run_engine("vector", e)

            @block.gpsimd
            def _(e):
                run_engine("gpsimd", e)

            @block.sync
            def _(e):
                run_engine("sync", e)
        self.wait_stats = stats


ATT_Q, ATT_KV, RET_QK, RET_V = 512, 128, 512, 1024
OFF_AQ = 0
OFF_AK = 512
OFF_AV = 640
OFF_RQ = 768
OFF_RK = 1280
OFF_RV = 1792
OFF_RG = 2816
OFF_GA = 3840
OFF_GB = 4864

CHUNKS = [("A1", 8, 512), ("A2", 8, 384), ("GA1", 8, 512), ("GA2", 8, 512), ("WA", 4, 1024)]
for _h in range(4):
    CHUNKS += [("RA%d" % _h, 8, 512), ("RB%d" % _h, 8, 256)]
CHUNKS += [("GB1", 8, 512), ("GB2", 8, 512), ("WB1", 8, 512), ("WB2", 8, 512), ("WO1", 8, 512), ("WO2", 8, 512)]
for _q in range(4):
    CHUNKS += [("UP%da" % _q, 8, 512), ("UP%db" % _q, 8, 512), ("DN%da" % _q, 8, 512), ("DN%db" % _q, 8, 512)]
CH_SIZE = [kc * n for (_, kc, n) in CHUNKS]
CH_OFF = [int(v) for v in np.concatenate([[0], np.cumsum(CH_SIZE)[:-1]])]
CH_IDX = {name: i for i, (name, _, _) in enumerate(CHUNKS)}
WTOT = int(sum(CH_SIZE))
NCH = len(CHUNKS)

C_ID, C_MGT, C_MLE, C_GMIX, C_GMLP, C_SINK, C_DQ, C_NDQ, C_DK, C_NDK, C_GF = 0, 128, 256, 384, 400, 416, 432, 436, 440, 444, 448
NCONST = 448 + 1024
C_EPSQ = C_NDQ
NTAB = 192


def _chunk(W, cols, kc_n):
    sub = W[:, cols]
    return np.ascontiguousarray(sub.reshape(kc_n, 128, len(cols)).transpose(1, 0, 2)).reshape(128, -1)


def prep_weights(w_in, w_a, w_b, w_out, w_up, w_down):
    L = w_in.shape[0]
    out = np.empty((L, 128, WTOT), np.float32)
    perm = np.concatenate([np.arange(0, 128, 2), np.arange(1, 128, 2)])
    ar = np.arange
    for l in range(L):
        parts = {}
        parts["A1"] = _chunk(w_in[l], OFF_AQ + ar(512), 8)
        kcols = np.concatenate([OFF_AK + ar(64), OFF_AK + ar(64), OFF_AK + 64 + ar(64), OFF_AK + 64 + ar(64),
                                OFF_AV + ar(128)])
        parts["A2"] = _chunk(w_in[l], kcols, 8)
        parts["GA1"] = _chunk(w_in[l], OFF_GA + ar(512), 8)
        parts["GA2"] = _chunk(w_in[l], OFF_GA + 512 + ar(512), 8)
        parts["WA"] = _chunk(w_a[l], ar(1024), 4)
        for h in range(4):
            cols = np.concatenate([OFF_RQ + h * 128 + perm, OFF_RK + h * 128 + perm, OFF_RV + h * 256 + ar(256)])
            parts["RA%d" % h] = _chunk(w_in[l], cols, 8)
            parts["RB%d" % h] = _chunk(w_in[l], OFF_RG + h * 256 + ar(256), 8)
        parts["GB1"] = _chunk(w_in[l], OFF_GB + ar(512), 8)
        parts["GB2"] = _chunk(w_in[l], OFF_GB + 512 + ar(512), 8)
        parts["WB1"] = _chunk(w_b[l], ar(512), 8)
        parts["WB2"] = _chunk(w_b[l], 512 + ar(512), 8)
        parts["WO1"] = _chunk(w_out[l], ar(512), 8)
        parts["WO2"] = _chunk(w_out[l], 512 + ar(512), 8)
        for q in range(4):
            parts["UP%da" % q] = _chunk(w_up[l], q * 1024 + ar(512), 8)
            parts["UP%db" % q] = _chunk(w_up[l], q * 1024 + 512 + ar(512), 8)
            parts["DN%da" % q] = _chunk(w_down[l][q * 1024:(q + 1) * 1024], ar(512), 8)
            parts["DN%db" % q] = _chunk(w_down[l][q * 1024:(q + 1) * 1024], 512 + ar(512), 8)
        for i, (name, kc, n) in enumerate(CHUNKS):
            out[l, :, CH_OFF[i]:CH_OFF[i] + CH_SIZE[i]] = parts[name]
    return out


def prep_consts(g_mix, sinks, g_mlp, g_final):
    c = np.zeros((128, NCONST), np.float32)
    c[:, C_ID:C_ID + 128] = np.eye(128, dtype=np.float32)
    k = np.arange(128)[:, None]
    q = np.arange(128)[None, :]
    c[:, C_MGT:C_MGT + 128] = (k > q).astype(np.float32)
    c[:, C_MLE:C_MLE + 128] = (k <= q).astype(np.float32)
    L = g_mix.shape[0]
    for l in range(L):
        c[:, C_GMIX + l * 8:C_GMIX + l * 8 + 8] = g_mix[l].reshape(8, 128).T
        c[:, C_GMLP + l * 8:C_GMLP + l * 8 + 8] = g_mlp[l].reshape(8, 128).T
        for g in range(2):
            for half in range(2):
                for cc in range(2):
                    head = 4 * g + 2 * cc + half
                    c[:, C_SINK + l * 8 + g * 4 + half * 2 + cc] = sinks[l, head]
    i = np.arange(128, dtype=np.float64)
    for h in range(4):
        gam = 1.0 - 2.0 ** (-5.0 - h)
        dq = gam ** (i + 1.0)
        dk = gam ** (-(i + 1.0)) * (128.0 ** -0.5)
        c[:, C_DQ + h] = dq
        c[:, C_EPSQ + h] = EPS / (dq * dq)
        c[:, C_DK + h] = dk
        c[:, C_NDK + h] = -dk
    c[:, C_GF:C_GF + 1024] = g_final[None, :]
    return c


def prep_tabs(S=4096):
    pos = np.arange(S, dtype=np.float32)
    inv = (10000.0 ** (-np.arange(32, dtype=np.float32) / 32)).astype(np.float32)
    angA = pos[:, None] * inv[None, :]
    theta = (1.0 / (10000.0 ** np.linspace(0.0, 1.0, 64, dtype=np.float32))).astype(np.float32)
    angR = pos[:, None] * theta[None, :]
    t = np.concatenate([np.cos(angA), np.sin(angA), np.cos(angR), np.sin(angR)], axis=1)
    return np.ascontiguousarray(t.astype(np.float32))


GAM128 = [float((1.0 - 2.0 ** (-5.0 - h)) ** 128) for h in range(4)]


def build_program(nblocks=4, nlayers=2, stop=None, dumps=()):
    nc = bass.Bass("TRN2", target_bir_lowering=False)
    S = nblocks * TBLK
    x_d = nc.dram_tensor("x", [S, D], F32, kind="ExternalInput").ap()
    w_d = nc.dram_tensor("wts", [2, 128, WTOT], F32, kind="ExternalInput").ap()
    c_d = nc.dram_tensor("consts", [128, NCONST], F32, kind="ExternalInput").ap()
    t_d = nc.dram_tensor("tabs", [4096, NTAB], F32, kind="ExternalInput").ap()
    o_d = nc.dram_tensor("out", [S, D], F32, kind="ExternalOutput").ap()

    es = ExitStack()

    def sb(name, shape, dt):
        return es.enter_context(nc.sbuf_tensor("s_" + name, shape, dt))

    cst = sb("cst", [128, NCONST], F32)
    identb = sb("identb", [128, 128], BF16)
    mask2b = sb("mask2b", [128, 2, 128], BF16)
    esink = sb("esink", [128, 16], F32)
    mhalf = sb("mhalf", [128, 8], F32)
    dummy = sb("dummyk", [128, 8], F32)
    xres = sb("xres", [128, NT, D], F32)
    hT = sb("hT", [128, 8, TBLK], BF16)
    wsl = sb("wsl", [128, 4, 4096], BF16)
    tabs = sb("tabs", [128, NT, NTAB], F32)
    xn = sb("xn", [128, 2, D], BF16)
    junk = sb("junk", [128, D], BF16)
    scr = sb("scr", [128, 2, 512], F32)
    qrope = sb("qrope", [128, 2, 512], BF16)
    krope = sb("krope", [128, 2, 256], BF16)
    g1 = sb("g1", [128, 16384], BF16)
    KTs = sb("KTs", [128, 2, 2, 128], BF16)
    Vs = sb("Vs", [128, 2, 2, 65], BF16)
    Ebuf = sb("Ebuf", [128, 2, 2, 512], BF16)
    Atok = sb("Atok", [128, 2, 512], BF16)
    small = sb("small", [128, 64], F32)
    sgu = sb("sgu", [128, 8, TBLK], BF16)
    m1T = sb("m1T", [128, 8, TBLK], BF16)
    Smb = sb("Smb", [128, 2, 128], BF16)
    rtok = sb("rtok", [128, 2, 256], BF16)
    Z = sb("Z", [128, 2, 4, 256], F32)
    Rb = sb("Rb", [128, 2, 256], BF16)
    rT = sb("rT", [128, 8, TBLK], BF16)
    psf = es.enter_context(nc.psum_tensor("psf", [128, 6, 512], F32))
    psb = es.enter_context(nc.psum_tensor("psb", [128, 2, 1024], BF16))

    QT = g1[:, 0:4096].rearrange("p (c t) -> p c t", c=4)
    AT = g1[:, 4096:8192].rearrange("p (c t) -> p c t", c=4)
    KT = g1[:, 8192:10240].rearrange("p (g t) -> p g t", g=2)
    Vaug = g1[:, 10240:10240 + 8 * 2 * 65].rearrange("p (t g e) -> p t g e", t=8, g=2)
    qkT = g1[:, 0:4096].rearrange("p (a b t) -> p a b t", a=2, b=2)
    qktok = g1[:, 4096:8192].rearrange("p (a t d) -> p a t d", a=2, t=8)
    vtok = g1[:, 8192:12288].rearrange("p (a t d) -> p a t d", a=2, t=8)
    gate = g1[:, 12288:16384].rearrange("p (a t d) -> p a t d", a=2, t=8)
    G1K = ("g1",)

    P = Prog(nc)

    def op(eng, method, reads, writes, name="", **kw):
        return P.add(eng, lambda e: getattr(e, method)(**kw), reads, writes, name)

    def mm(items, reads, writes, name=""):
        def fn(e):
            ins = None
            for (o, l, r, st_, sp) in items:
                ins = e.matmul(o, l, r, start=st_, stop=sp)
            return ins
        return P.add("tensor", fn, reads, writes, name)

    def tr(items, reads, writes, name=""):
        def fn(e):
            ins = None
            for (o, i_) in items:
                ins = e.transpose(o, i_, identb[:])
            return ins
        return P.add("tensor", fn, reads + ["identb"], writes, name)

    st = {"pf": 0, "pb": 0, "sm": 0, "scr": 0}

    def pf():
        i = st["pf"]
        st["pf"] = (i + 1) % 6
        return psf[:, i, :], ("pf", i)

    def pb():
        i = st["pb"]
        st["pb"] = (i + 1) % 2
        return psb[:, i, :], ("pb", i)

    def sm(n):
        i = st["sm"]
        if i + n > 64:
            i = 0
        st["sm"] = i + n
        return small[:, i:i + n], [("sm", j) for j in range(i, i + n)]

    seq = [(l, ci) for b in range(nblocks) for l in range(nlayers) for ci in range(NCH)]
    wst = {"next_load": 0, "next_rel": 0, "done": set()}

    def w_issue():
        n = wst["next_load"]
        if n >= len(seq):
            return
        wst["next_load"] = n + 1
        l, ci = seq[n]
        slot = n % 4
        size = CH_SIZE[ci]
        src = w_d[l, :, CH_OFF[ci]:CH_OFF[ci] + size]
        dst = wsl[:, slot, 0:size]
        P.dma("gpsimd", lambda e: [e.dma_start(out=dst, in_=src, max_dma_last_dim=4096)], "w%d" % slot, 1,
              writes=[("w", slot)], name="wload%d" % n)

    def w_get(name, b, l):
        n = (b * nlayers + l) * NCH + CH_IDX[name]
        assert n < wst["next_load"], ("chunk not loaded yet (ring too small for this interleaving)", name, n)
        assert n >= wst["next_rel"]
        slot = n % 4
        kc, ncols = CHUNKS[CH_IDX[name]][1], CHUNKS[CH_IDX[name]][2]
        view = wsl[:, slot, 0:kc * ncols].rearrange("p (k n) -> p k n", k=kc)
        return view, ("w", slot), n

    def w_done(n):
        wst["done"].add(n)
        while wst["next_rel"] in wst["done"]:
            wst["done"].remove(wst["next_rel"])
            wst["next_rel"] += 1
            w_issue()

    P.dma("sync", lambda e: [e.dma_start(out=cst[:], in_=c_d[:, :])], "cstl", 1, writes=["cst"])
    op("vector", "tensor_copy", ["cst"], ["identb"], out=identb[:], in_=cst[:, C_ID:C_ID + 128])
    op("vector", "tensor_copy", ["cst"], ["mask2b"], out=mask2b[:, 0, :], in_=cst[:, C_MGT:C_MGT + 128])
    op("vector", "tensor_copy", ["cst", "mask2b"], ["mask2b"], out=mask2b[:, 1, :], in_=cst[:, C_MLE:C_MLE + 128])
    op("scalar", "activation", ["cst"], ["esink"], out=esink[:], in_=cst[:, C_SINK:C_SINK + 16], func=AF.Exp)
    op("vector", "memset", [], [("Z", l, h) for l in range(2) for h in range(4)], ap=Z[:], constant=0.0)
    op("gpsimd", "memset", [], ["mhalf"], ap=mhalf[:], constant=-0.5)
    for _ in range(4):
        w_issue()

    xkeys = lambda t: [("x", t, 0), ("x", t, 1)]

    def rstd_from_ms(ms_ap, ms_keys, eps_ap=None):
        v, vk = sm(1)
        r, rk = sm(1)
        if eps_ap is None:
            op("gpsimd", "tensor_scalar", ms_keys, vk, out=v, in0=ms_ap, scalar1=EPS, scalar2=None, op0=ALU.add)
        else:
            op("gpsimd", "tensor_tensor", ms_keys + ["cst"], vk, out=v, in0=ms_ap, in1=eps_ap, op=ALU.add)
        op("gpsimd", "tensor_tensor", vk + ["mhalf"], rk, out=r, in0=v, in1=mhalf[:, 0:1], op=ALU.pow)
        return r, rk

    def norm_pre(t):
        ms, msk = sm(1)
        op("scalar", "activation", xkeys(t), msk, out=junk[:], in_=xres[:, t, :], func=AF.Square,
           scale=1.0 / 32.0, accum_out=ms)
        r, rk = rstd_from_ms(ms, msk)
        par = t % 2
        op("scalar", "activation", xkeys(t) + rk, [("xn", par)], out=xn[:, par, :], in_=xres[:, t, :], func=AF.Copy,
           scale=r)

    def norm_post(t, gcol):
        par = t % 2
        bank, bk = pb()
        tr([(bank[:, kc * 128:(kc + 1) * 128], xn[:, par, kc * 128:(kc + 1) * 128]) for kc in range(8)],
           [("xn", par)], [bk])
        gb = cst[:, gcol:gcol + 8].unsqueeze(2).broadcast_to([128, 8, 128])
        op("vector", "tensor_tensor", [bk, "cst"], [("hT", t)], out=hT[:, :, t * 128:(t + 1) * 128],
           in0=bank.rearrange("p (k t) -> p k t", k=8), in1=gb, op=ALU.mult)

    def rope(xa, nh, half, cosb, sinb, sc, out_ap, rk, wk, scr_off):
        n = nh * 2 * half
        u = scr[:, 0, scr_off:scr_off + n]
        tt = scr[:, 1, scr_off:scr_off + n]
        xv = xa.rearrange("p (h two d) -> p h two d", two=2, d=half)
        uv = u.rearrange("p (h two d) -> p h two d", two=2, d=half)
        tv = tt.rearrange("p (h two d) -> p h two d", two=2, d=half)
        sbc = cosb.unsqueeze(1).broadcast_to([128, nh, half])
        sbb = sinb.unsqueeze(1).broadcast_to([128, nh, half])
        offs = [o for o in (0, 128) if (o == scr_off) or (scr_off < o < scr_off + n)]
        ku = [("scr", 0, o) for o in offs]
        kt_ = [("scr", 1, o) for o in offs]
        for j in range(2):
            op("vector", "scalar_tensor_tensor", rk + ku, ku, out=uv[:, :, j, :], in0=xv[:, :, j, :], scalar=sc, in1=sbc,
               op0=ALU.mult, op1=ALU.mult)
        op("vector", "scalar_tensor_tensor", rk + kt_, kt_, out=tv[:, :, 0, :], in0=xv[:, :, 1, :],
           scalar=-sc, in1=sbb, op0=ALU.mult, op1=ALU.mult)
        op("vector", "scalar_tensor_tensor", rk + kt_, kt_, out=tv[:, :, 1, :], in0=xv[:, :, 0, :],
           scalar=sc, in1=sbb, op0=ALU.mult, op1=ALU.mult)
        op("vector", "tensor_tensor", ku + kt_, wk, out=out_ap, in0=u, in1=tt, op=ALU.add)

    def barrier_g1():
        op("vector", "memset", [], [G1K], ap=dummy[:, 0:1], constant=0.0)

    def add_x(bank, bk, t, ch, scale=None):
        if scale is None:
            op("vector", "tensor_tensor", [bk, ("x", t, ch)], [("x", t, ch)], out=xres[:, t, ch * 512:(ch + 1) * 512],
               in0=bank[:, :], in1=xres[:, t, ch * 512:(ch + 1) * 512], op=ALU.add)
        else:
            op("vector", "scalar_tensor_tensor", [bk, ("x", t, ch)], [("x", t, ch)],
               out=xres[:, t, ch * 512:(ch + 1) * 512], in0=bank[:, :], scalar=scale,
               in1=xres[:, t, ch * 512:(ch + 1) * 512], op0=ALU.mult, op1=ALU.add)

    def gate_group(wg, kg, base, fc, half):
        bank, bk = pf()
        mm([(bank[:, :], wg[:, kc, fc * 128:(fc + 1) * 128], hT[:, kc, half * 512:(half + 1) * 512],
             kc == 0, kc == 7) for kc in range(8)],
           [("hT", tt_) for tt_ in range(half * 4, half * 4 + 4)] + [kg], [bk])
        op("scalar", "activation", [bk], [("sgu", base + fc, half)],
           out=sgu[:, base + fc, half * 512:(half + 1) * 512], in_=bank[:, :], func=AF.Tanh, scale=0.5)

    def layer(b, l, norm_inline, tail_kind, next_gcol):
        first_blk = (b == 0)
        gcol1 = C_GMIX + l * 8
        barrier_g1()
        op("vector", "memset", [G1K], [("V", t) for t in range(NT)], ap=Vaug[:, :, :, 64:65], constant=1.0)
        wA1, kA1, nA1 = w_get("A1", b, l)
        pend = []
        if norm_inline:
            norm_pre(0)
            norm_pre(1)
        for t in range(NT):
            if norm_inline:
                norm_post(t, gcol1)
                if t + 2 < NT:
                    norm_pre(t + 2)
            bank, bk = pf()
            mm([(bank[:, 0:512], hT[:, kc, t * 128:(t + 1) * 128], wA1[:, kc, :], kc == 0, kc == 7) for kc in range(8)],
               [("hT", t), kA1], [bk])
            par = t % 2
            rope(bank[:, 0:512], 8, 32, tabs[:, t, 0:32], tabs[:, t, 32:64], 0.125, qrope[:, par, :],
                 [bk, "tabs"], [("qrope", par)], 0)

            def qtr(t=t, par=par):
                tb, tbk = pb()
                tr([(tb[:, c * 128:(c + 1) * 128], qrope[:, par, c * 128:(c + 1) * 128]) for c in range(4)],
                   [("qrope", par)], [tbk])
                op("scalar", "activation", [tbk, G1K], [("QT", t)], out=QT[:, :, t * 128:(t + 1) * 128],
                   in_=tb[:, 0:512].rearrange("p (c t) -> p c t", c=4), func=AF.Copy)
            if pend:
                pend.pop(0)()
            pend.append(qtr)
        w_done(nA1)
        wA2, kA2, nA2 = w_get("A2", b, l)
        for t in range(NT):
            bank, bk = pf()
            mm([(bank[:, 0:384], hT[:, kc, t * 128:(t + 1) * 128], wA2[:, kc, :], kc == 0, kc == 7) for kc in range(8)],
               [("hT", t), kA2], [bk])
            par = t % 2
            rope(bank[:, 0:256], 4, 32, tabs[:, t, 0:32], tabs[:, t, 32:64], 1.0, krope[:, par, :],
                 [bk, "tabs"], [("krope", par)], 0)
            op("scalar", "activation", [bk, G1K, ("V", t)], [("V", t)], out=Vaug[:, t, :, 0:64],
               in_=bank[:, 256:384].rearrange("p (g d) -> p g d", g=2), func=AF.Copy)

            def ktr(t=t, par=par):
                tb, tbk = pb()
                tr([(tb[:, g * 128:(g + 1) * 128], krope[:, par, g * 128:(g + 1) * 128]) for g in range(2)],
                   [("krope", par)], [tbk])
                op("scalar", "activation", [tbk, G1K], [("KT", t)], out=KT[:, :, t * 128:(t + 1) * 128],
                   in_=tb[:, 0:256].rearrange("p (g t) -> p g t", g=2), func=AF.Copy)
            if pend:
                pend.pop(0)()
            pend.append(ktr)
        w_done(nA2)
        while pend:
            pend.pop(0)()
        units = [(t, g) for t in range(NT) for g in range(2)]

        def stage_S(t, g, u):
            noprev = first_blk and t == 0
            kbs = [1] if noprev else [0, 1]
            banks = []
            for half in range(2):
                bank, bk = pf()
                banks.append((bank, bk))
                items = []
                reads = [("QT", t), G1K]
                for kb in kbs:
                    if kb == 1:
                        ksrc = KT[half * 64:(half + 1) * 64, g, t * 128:(t + 1) * 128]
                        reads.append(("KT", t))
                    elif t > 0:
                        ksrc = KT[half * 64:(half + 1) * 64, g, (t - 1) * 128:t * 128]
                        reads.append(("KT", t - 1))
                    else:
                        ksrc = KTs[half * 64:(half + 1) * 64, l, g, :]
                        reads.append(("KTs", l))
                    items.append((bank[:, kb * 256:(kb + 1) * 256].rearrange("p (c q) -> p c q", c=2), ksrc,
                                  QT[half * 64:(half + 1) * 64, 2 * g:2 * g + 2, t * 128:(t + 1) * 128], True, True))
                mm(items, reads, [bk])
            ue = u % 2
            for half in range(2):
                bank, bk = banks[half]
                lo = 256 if noprev else 0
                op("scalar", "activation", [bk], [("E", ue, half)], out=Ebuf[:, ue, half, lo:512], in_=bank[:, lo:512],
                   func=AF.Exp)
                if noprev:
                    ev = Ebuf[:, ue, half, 256:512].rearrange("p (c q) -> p c q", c=2)
                    mk = mask2b[:, 1, :].unsqueeze(1).broadcast_to([128, 2, 128])
                else:
                    ev = Ebuf[:, ue, half, :].rearrange("p (k c q) -> p k c q", k=2, c=2)
                    mk = mask2b[:, :, :].unsqueeze(2).broadcast_to([128, 2, 2, 128])
                op("vector", "tensor_tensor", [("E", ue, half), "mask2b"], [("E", ue, half)], out=ev, in0=ev, in1=mk,
                   op=ALU.mult)

        def stage_PV(t, g, u):
            noprev = first_blk and t == 0
            kbs = [1] if noprev else [0, 1]
            ue = u % 2
            bank, bk = pf()
            items = []
            reads = [("E", ue, 0), ("E", ue, 1), G1K]
            for half in range(2):
                for cc in range(2):
                    jj = half * 2 + cc
                    for i_, kb in enumerate(kbs):
                        if kb == 1:
                            vsrc = Vaug[:, t, g, :]
                            reads.append(("V", t))
                        elif t > 0:
                            vsrc = Vaug[:, t - 1, g, :]
                            reads.append(("V", t - 1))
                        else:
                            vsrc = Vs[:, l, g, :]
                            reads.append(("Vs", l))
                        items.append((bank[:, jj * 65:(jj + 1) * 65],
                                      Ebuf[:, ue, half, kb * 256 + cc * 128:kb * 256 + (cc + 1) * 128], vsrc,
                                      i_ == 0, i_ == len(kbs) - 1))
            mm(items, reads, [bk])
            ov = bank[:, 0:260].rearrange("p (h c e) -> p h c e", h=2, c=2)
            den, dk_ = sm(4)
            rden, rk_ = sm(4)
            op("vector", "tensor_tensor", [bk, "esink"], dk_, out=den.rearrange("p (h c) -> p h c", h=2),
               in0=ov[:, :, :, 64], in1=esink[:, l * 8 + g * 4:l * 8 + g * 4 + 4].rearrange("p (h c) -> p h c", h=2),
               op=ALU.add)
            op("vector", "reciprocal", dk_, rk_, out=rden, in_=den)
            ap_ = t % 2
            ao = Atok[:, ap_, g * 256:(g + 1) * 256].rearrange("p (c h d) -> p h c d", c=2, h=2)
            rb_ = rden.rearrange("p (h c) -> p h c", h=2).unsqueeze(3).broadcast_to([128, 2, 2, 64])
            op("vector", "tensor_tensor", [bk] + rk_, [("Atok", ap_, g)], out=ao, in0=ov[:, :, :, 0:64], in1=rb_,
               op=ALU.mult)

        def stage_T(t):
            ap_ = t % 2
            tb, tbk = pb()
            tr([(tb[:, c * 128:(c + 1) * 128], Atok[:, ap_, c * 128:(c + 1) * 128]) for c in range(4)],
               [("Atok", ap_, 0), ("Atok", ap_, 1)], [tbk])
            op("scalar", "activation", [tbk, G1K], [("AT", t)], out=AT[:, :, t * 128:(t + 1) * 128],
               in_=tb[:, 0:512].rearrange("p (c t) -> p c t", c=4), func=AF.Copy)

        ga_groups = [(nm, base, fc, half) for (nm, base) in (("GA1", 0), ("GA2", 4)) for fc in range(4) for half in range(2)]
        ga_w = {}
        for i in range(len(units) + 3):
            if i < len(units):
                stage_S(units[i][0], units[i][1], i)
            if i < len(ga_groups):
                nm, base, fc, half = ga_groups[i]
                if nm not in ga_w:
                    ga_w[nm] = w_get(nm, b, l)
                wg, kg, ng = ga_w[nm]
                gate_group(wg, kg, base, fc, half)
                if fc == 3 and half == 1:
                    w_done(ng)
            j = i - 1
            if 0 <= j < len(units):
                stage_PV(units[j][0], units[j][1], j)
            k = i - 3
            if 0 <= k < len(units) and units[k][1] == 1:
                stage_T(units[k][0])
        op("vector", "tensor_copy", [("KT", 7), G1K], [("KTs", l)], out=KTs[:, l, :, :], in_=KT[:, :, 7 * 128:8 * 128])
        op("vector", "tensor_copy", [("V", 7), G1K], [("Vs", l)], out=Vs[:, l, :, :], in_=Vaug[:, 7, :, :])
        wa, ka, na = w_get("WA", b, l)
        for half in range(2):
            for c in range(8):
                bank, bk = pf()
                mm([(bank[:, :], wa[:, kc, c * 128:(c + 1) * 128], AT[:, kc, half * 512:(half + 1) * 512],
                     kc == 0, kc == 3) for kc in range(4)],
                   [("AT", tt_) for tt_ in range(half * 4, half * 4 + 4)] + [ka, G1K], [bk])
                op("vector", "scalar_tensor_tensor", [bk, ("sgu", c, half)], [("m1T", c, half)],
                   out=m1T[:, c, half * 512:(half + 1) * 512], in0=sgu[:, c, half * 512:(half + 1) * 512], scalar=1.0,
                   in1=bank[:, :], op0=ALU.add, op1=ALU.mult)
        w_done(na)
        barrier_g1()
        wts_h = {}

        def A_ra(h, t):
            par = h % 2
            if t == 0:
                wts_h[h] = (w_get("RA%d" % h, b, l), w_get("RB%d" % h, b, l))
            (wra, kra, _), _ = wts_h[h]
            bank, bk = pf()
            mm([(bank[:, 0:512], hT[:, kc, t * 128:(t + 1) * 128], wra[:, kc, :], kc == 0, kc == 7) for kc in range(8)],
               [("hT", t), kra], [bk])
            return bank, bk

        def A_rope(h, t, bank, bk):
            par = h % 2
            rope(bank[:, 0:256], 2, 64, tabs[:, t, 64:128], tabs[:, t, 128:192], 1.0, qktok[:, par, t, :],
                 [bk, "tabs", G1K], [("qktok", par, t)], 128)
            op("scalar", "activation", [bk, G1K, "cst"], [("vtok", par, t)], out=vtok[:, par, t, :], in_=bank[:, 256:512],
               func=AF.Copy, scale=cst[:, C_DK + h:C_DK + h + 1])

        def A_rb(h, t):
            par = h % 2
            _, (wrb, krb, _) = wts_h[h]
            bank, bk = pf()
            mm([(bank[:, 0:256], hT[:, kc, t * 128:(t + 1) * 128], wrb[:, kc, :], kc == 0, kc == 7) for kc in range(8)],
               [("hT", t), krb], [bk])
            op("scalar", "activation", [bk, G1K], [("gate", par, t)], out=gate[:, par, t, :], in_=bank[:, 0:256],
               func=AF.Silu)
            if t == NT - 1:
                w_done(wts_h[h][0][2])
                w_done(wts_h[h][1][2])

        def A_tr(h, t):
            par = h % 2
            tb, tbk = pb()
            tr([(tb[:, 0:128], qktok[:, par, t, 0:128]), (tb[:, 128:256], qktok[:, par, t, 128:256])],
               [("qktok", par, t), G1K], [tbk])
            op("scalar", "activation", [tbk, G1K], [("qkT", par, t)], out=qkT[:, par, :, t * 128:(t + 1) * 128],
               in_=tb[:, 0:256].rearrange("p (a t) -> p a t", a=2), func=AF.Copy)

        def B_s_kv(h, t):
            par = h % 2
            first = first_blk and t == 0
            sp = (h * 8 + t) % 2
            bank, bk = pf()
            mm([(bank[:, 0:128], qkT[:, par, 1, t * 128:(t + 1) * 128], qkT[:, par, 0, t * 128:(t + 1) * 128], True, True)],
               [("qkT", par, t), G1K], [bk])
            kvb, kvk = pf()
            mm([(kvb[:, 0:256], qktok[:, par, t, 128:256], vtok[:, par, t, :], True, True)],
               [("qktok", par, t), ("vtok", par, t), G1K], [kvk])
            op("vector", "tensor_tensor", [bk, "cst"], [("Sm", sp)], out=Smb[:, sp, :], in0=bank[:, 0:128],
               in1=cst[:, C_MLE:C_MLE + 128], op=ALU.mult)
            if (not first) and t == 0:
                op("scalar", "activation", [("Z", l, h)], [("Rb", 0)], out=Rb[:, 0, :], in_=Z[:, l, h, :], func=AF.Copy,
                   scale=GAM128[h])
            return sp, kvb, kvk

        def B_r(h, t, sp, kvb, kvk):
            par = h % 2
            first = first_blk and t == 0
            rbp = t % 2
            bank, bk = pf()
            items = [(bank[:, 0:256], Smb[:, sp, :], vtok[:, par, t, :], True, first)]
            reads = [("Sm", sp), ("vtok", par, t), G1K]
            if not first:
                items.append((bank[:, 0:256], qkT[:, par, 0, t * 128:(t + 1) * 128], Rb[:, rbp, :], False, True))
                reads += [("qkT", par, t), ("Rb", rbp)]
            mm(items, reads, [bk])
            op("vector", "scalar_tensor_tensor", [kvk, ("Z", l, h)], [("Z", l, h)], out=Z[:, l, h, :], in0=Z[:, l, h, :],
               scalar=GAM128[h], in1=kvb[:, 0:256], op0=ALU.mult, op1=ALU.add)
            if t < NT - 1:
                nb_ = (t + 1) % 2
                op("scalar", "activation", [("Z", l, h)], [("Rb", nb_)], out=Rb[:, nb_, :], in_=Z[:, l, h, :],
                   func=AF.Copy, scale=GAM128[h])
            ms, msk = sm(1)
            op("scalar", "activation", [bk], msk, out=junk[:, 0:256], in_=bank[:, 0:256], func=AF.Square,
               scale=1.0 / 16.0, accum_out=ms)
            r, rk = rstd_from_ms(ms, msk, eps_ap=cst[:, C_EPSQ + h:C_EPSQ + h + 1])
            rp = (h * 8 + t) % 2
            op("vector", "scalar_tensor_tensor", [bk, ("gate", par, t), G1K] + rk, [("rtok", rp)], out=rtok[:, rp, :],
               in0=bank[:, 0:256], scalar=r, in1=gate[:, par, t, :], op0=ALU.mult, op1=ALU.mult)
            return rp

        def B_tr(h, t, rp):
            tb, tbk = pb()
            tr([(tb[:, 0:128], rtok[:, rp, 0:128]), (tb[:, 128:256], rtok[:, rp, 128:256])], [("rtok", rp)], [tbk])
            op("scalar", "activation", [tbk], [("rT", h, t)], out=rT[:, 2 * h:2 * h + 2, t * 128:(t + 1) * 128],
               in_=tb[:, 0:256].rearrange("p (a t) -> p a t", a=2), func=AF.Copy)

        gb_groups = [(nm, base, fc, half) for (nm, base) in (("GB1", 0), ("GB2", 4)) for fc in range(4) for half in range(2)]
        gb_w = {}

        def gb_fill(i):
            nm, base, fc, half = gb_groups[i]
            if nm not in gb_w:
                gb_w[nm] = w_get(nm, b, l)
            wg, kg, ng = gb_w[nm]
            gate_group(wg, kg, base, fc, half)
            if fc == 3 and half == 1:
                w_done(ng)

        pend_btr = []
        pend_atr = None
        for h in range(5):
            for t in range(NT):
                ab = None
                if h < 4:
                    ab = A_ra(h, t)
                else:
                    gb_fill(2 * t)
                bs = None
                if h >= 1:
                    bs = B_s_kv(h - 1, t)
                if len(pend_btr) >= 2:
                    B_tr(*pend_btr.pop(0))
                if h < 4:
                    A_rb(h, t)
                    A_rope(h, t, *ab)
                else:
                    gb_fill(2 * t + 1)
                if h >= 1:
                    rp = B_r(h - 1, t, *bs)
                    pend_btr.append((h - 1, t, rp))
                if pend_atr is not None:
                    A_tr(*pend_atr)
                    pend_atr = None
                if h < 4:
                    pend_atr = (h, t)
                    if t == NT - 1:
                        A_tr(*pend_atr)
                        pend_atr = None
        for nm, base in (("WB1", 0), ("WB2", 4)):
            wb_, kb_, nb_w = w_get(nm, b, l)
            for half in range(2):
                for fc in range(4):
                    c = base + fc
                    bank, bk = pf()
                    mm([(bank[:, :], wb_[:, kc, fc * 128:(fc + 1) * 128], rT[:, kc, half * 512:(half + 1) * 512],
                         kc == 0, kc == 7) for kc in range(8)],
                       [("rT", hh, tt_) for hh in range(4) for tt_ in range(half * 4, half * 4 + 4)] + [kb_], [bk])
                    while pend_btr:
                        B_tr(*pend_btr.pop(0))
                    sp_ = st["scr"]
                    st["scr"] = 1 - sp_
                    skey = ("scr", sp_, 0)
                    op("vector", "scalar_tensor_tensor", [bk, ("sgu", c, half)], [skey, ("scr", sp_, 128)],
                       out=scr[:, sp_, :], in0=sgu[:, c, half * 512:(half + 1) * 512], scalar=1.0, in1=bank[:, :],
                       op0=ALU.add, op1=ALU.mult)
                    op("vector", "tensor_tensor", [skey, ("m1T", c, half)], [("m1T", c, half)],
                       out=m1T[:, c, half * 512:(half + 1) * 512], in0=scr[:, sp_, :],
                       in1=m1T[:, c, half * 512:(half + 1) * 512], op=ALU.add)
            w_done(nb_w)
        gcol2 = C_GMLP + l * 8
        wo1, ko1, no1 = w_get("WO1", b, l)
        wo2, ko2, no2 = w_get("WO2", b, l)
        pend_post = []
        for t in range(NT):
            for ch, (wo, ko) in enumerate(((wo1, ko1), (wo2, ko2))):
                bank, bk = pf()
                mm([(bank[:, :], m1T[:, kc, t * 128:(t + 1) * 128], wo[:, kc, :], kc == 0, kc == 7) for kc in range(8)],
                   [("m1T", kc, t // 4) for kc in range(8)] + [ko], [bk])
                add_x(bank, bk, t, ch, scale=0.5)
            norm_pre(t)
            if pend_post:
                norm_post(pend_post.pop(0), gcol2)
            pend_post.append(t)
        w_done(no1)
        w_done(no2)
        for q in range(4):
            gi = 0
            for nm, base in (("UP%da" % q, 0), ("UP%db" % q, 4)):
                wu, ku, nu = w_get(nm, b, l)
                for half in range(2):
                    for fc in range(4):
                        bank, bk = pf()
                        mm([(bank[:, :], wu[:, kc, fc * 128:(fc + 1) * 128], hT[:, kc, half * 512:(half + 1) * 512],
                             kc == 0, kc == 7) for kc in range(8)],
                           [("hT", tt_) for tt_ in range(half * 4, half * 4 + 4)] + [ku], [bk])
                        gi += 1
                        if pend_post and gi == 2:
                            norm_post(pend_post.pop(0), gcol2)
                        sp_ = st["scr"]
                        st["scr"] = 1 - sp_
                        skey = ("scr", sp_, 0)
                        op("scalar", "activation", [bk], [skey, ("scr", sp_, 128)], out=scr[:, sp_, :], in_=bank[:, :],
                           func=AF.Copy)
                        op("vector", "scalar_tensor_tensor", [bk, skey], [("sgu", base + fc, half)],
                           out=sgu[:, base + fc, half * 512:(half + 1) * 512], in0=bank[:, :], scalar=0.0,
                           in1=scr[:, sp_, :], op0=ALU.max, op1=ALU.mult)
                w_done(nu)
            if q < 3:
                for nm, ch in (("DN%da" % q, 0), ("DN%db" % q, 1)):
                    wd, kd, nd = w_get(nm, b, l)
                    for t in range(NT):
                        bank, bk = pf()
                        mm([(bank[:, :], sgu[:, fc, t * 128:(t + 1) * 128], wd[:, fc, :], fc == 0, fc == 7)
                            for fc in range(8)], [("sgu", fc, t // 4) for fc in range(8)] + [kd], [bk])
                        add_x(bank, bk, t, ch)
                    w_done(nd)
            else:
                wd1, kd1, nd1 = w_get("DN3a", b, l)
                wd2, kd2, nd2 = w_get("DN3b", b, l)
                pend_n = []
                for t in range(NT):
                    for ch, (wd, kd) in enumerate(((wd1, kd1), (wd2, kd2))):
                        bank, bk = pf()
                        mm([(bank[:, :], sgu[:, fc, t * 128:(t + 1) * 128], wd[:, fc, :], fc == 0, fc == 7)
                            for fc in range(8)], [("sgu", fc, t // 4) for fc in range(8)] + [kd], [bk])
                        add_x(bank, bk, t, ch)
                    if tail_kind == "norm":
                        norm_pre(t)
                        if pend_n:
                            norm_post(pend_n.pop(0), next_gcol)
                        pend_n.append(t)
                    else:
                        ms, msk = sm(1)
                        op("scalar", "activation", xkeys(t), msk, out=junk[:], in_=xres[:, t, :], func=AF.Square,
                           scale=1.0 / 32.0, accum_out=ms)
                        r, rk = rstd_from_ms(ms, msk)
                        op("vector", "scalar_tensor_tensor", xkeys(t) + rk + ["cst"], xkeys(t), out=xres[:, t, :],
                           in0=xres[:, t, :], scalar=r, in1=cst[:, C_GF:C_GF + 1024], op0=ALU.mult, op1=ALU.mult)
                        r0 = (b * NT + t) * 128
                        P.dma("sync", lambda e, t=t, r0=r0: [e.dma_start(out=o_d[r0:r0 + 128, :], in_=xres[:, t, :])],
                              "xs%d" % t, 1, reads=xkeys(t), writes=[("out", b, t)])
                w_done(nd1)
                w_done(nd2)
                while pend_n:
                    norm_post(pend_n.pop(0), next_gcol)

    for b in range(nblocks):
        P.epoch = b
        for t in range(NT):
            r0 = (b * NT + t) * 128
            P.dma("sync", lambda e, t=t, r0=r0: [e.dma_start(out=xres[:, t, :], in_=x_d[r0:r0 + 128, :])], "xl%d" % t, 1,
                  writes=xkeys(t))
        P.dma("sync", lambda e, b=b: [e.dma_start(out=tabs[:], in_=t_d[b * TBLK:(b + 1) * TBLK, :].rearrange(
            "(t p) c -> p t c", p=128))], "tbl", 1, writes=["tabs"])
        for l in range(nlayers):
            last = (l == nlayers - 1)
            layer(b, l, norm_inline=(l == 0), tail_kind=("final" if last else "norm"),
                  next_gcol=(None if last else C_GMIX + (l + 1) * 8))
    outk = [k for k in P.last_writer.keys() if isinstance(k, tuple) and k[0] in ("out",)]
    outk += [("w", s_) for s_ in range(4)] + ["tabs", "cst"]
    P.add("sync", lambda e: None, reads=outk, writes=[], name="final")

    sem_names = P.sem_names()
    sems = {n: es.enter_context(nc.semaphore(n)) for n in sem_names}
    P.emit(sems)
    es.close()
    return nc, P


_CACHE = {}


def kernel(x, g_mix, w_in, sinks, w_a, w_b, w_out, g_mlp, w_up, w_down, g_final):
    x = np.asarray(x, np.float32)
    B = x.shape[0]
    wts = prep_weights(np.asarray(w_in, np.float32), np.asarray(w_a, np.float32), np.asarray(w_b, np.float32),
                       np.asarray(w_out, np.float32), np.asarray(w_up, np.float32), np.asarray(w_down, np.float32))
    consts = prep_consts(np.asarray(g_mix, np.float32), np.asarray(sinks, np.float32), np.asarray(g_mlp, np.float32),
                         np.asarray(g_final, np.float32))
    tabs = prep_tabs(4096)
    nc, _ = build_program(4, 2)
    in_maps = [{"x": np.ascontiguousarray(x[b]), "wts": wts, "consts": consts, "tabs": tabs} for b in range(B)]
    res = run_bass_kernel_spmd(nc, in_maps, core_ids=list(range(B)))
    out = np.stack([np.asarray(r["out"], np.float32) for r in res.results], axis=0)
    return out
```

```python
import os
import numpy as np
from contextlib import ExitStack
import concourse.bass as bass
import concourse.mybir as mybir
from concourse.bass_utils import run_bass_kernel_spmd

F32 = mybir.dt.float32
BF16 = mybir.dt.bfloat16
AF = mybir.ActivationFunctionType
SIGF = getattr(AF, os.environ.get('KDBG_SIG', 'Sigmoid'))
ALU = mybir.AluOpType

D = 1024
NT = 8
TBLK = 1024
EPS = 1e-6
ENGINES = ("tensor", "scalar", "vector", "gpsimd", "sync")


class Op:
    __slots__ = ("eng", "fn", "reads", "writes", "dma_sem", "ndma", "idx", "deps",
                 "need_sig", "count", "sem", "name", "epoch", "true_writes", "raw")

    def __init__(self, eng, fn, reads, writes, dma_sem=None, ndma=0, name="", epoch=0):
        self.eng = eng
        self.fn = fn
        self.reads = reads
        self.writes = writes
        self.dma_sem = dma_sem
        self.ndma = ndma
        self.deps = []
        self.need_sig = False
        self.count = None
        self.sem = None
        self.name = name
        self.epoch = epoch


class Prog:
    def __init__(self, nc):
        self.nc = nc
        self.ops = []
        self.last_writer = {}
        self.readers = {}
        self.dma_counts = {}
        self.epoch = 0

    def add(self, eng, fn, reads=(), writes=(), name=""):
        reads = list(reads)
        writes = list(writes)
        op_tw = set(writes)
        for k in reads:
            if isinstance(k, tuple) and k[0] in ("pf", "pb") and k not in writes:
                writes.append(k)
        op = Op(eng, fn, reads, writes, name=name, epoch=self.epoch)
        op.true_writes = op_tw
        self._track(op)
        return op

    def dma(self, eng, fn, sem_name, ndma, reads=(), writes=(), name=""):
        op = Op(eng, fn, list(reads), list(writes), dma_sem=sem_name, ndma=ndma, name=name,
                epoch=self.epoch)
        op.true_writes = set(op.writes)
        self._track(op)
        return op

    def _track(self, op):
        op.idx = len(self.ops)
        deps = {}
        for k in op.reads:
            w = self.last_writer.get(k)
            if w is not None:
                deps[w.idx] = w
        for k in op.writes:
            w = self.last_writer.get(k)
            if w is not None:
                deps[w.idx] = w
            for r in self.readers.get(k, ()):
                deps[r.idx] = r
        for k in op.reads:
            self.readers.setdefault(k, []).append(op)
        for k in op.writes:
            self.last_writer[k] = op
            self.readers[k] = []
        deps.pop(op.idx, None)
        op.deps = list(deps.values())
        rset = set(op.reads)
        op.raw = set(d.idx for d in op.deps if d.true_writes & rset)
        self.ops.append(op)

    def sem_names(self):
        names = set()
        for op in self.ops:
            if op.dma_sem is not None:
                names.add(op.dma_sem)
            else:
                names.add("e_%s_%d" % (op.eng, op.epoch))
        return sorted(names)

    def emit(self, sems):
        nc = self.nc

        def skip(d, op):
            if d.dma_sem is not None or op.dma_sem is not None or d.eng != op.eng:
                return False
            if d.eng == "tensor":
                return True
            return d.idx not in op.raw

        for op in self.ops:
            for d in op.deps:
                if d.dma_sem is None and not skip(d, op):
                    d.need_sig = True
        counters = {}
        for op in self.ops:
            if op.dma_sem is not None:
                c = self.dma_counts.get(op.dma_sem, 0) + 16 * op.ndma
                self.dma_counts[op.dma_sem] = c
                op.count = c
                op.sem = op.dma_sem
            elif op.need_sig:
                s = "e_%s_%d" % (op.eng, op.epoch)
                counters[s] = counters.get(s, 0) + 1
                op.count = counters[s]
                op.sem = s
        self.max_counts = counters
        per_eng = {e: [] for e in ENGINES}
        for op in self.ops:
            per_eng[op.eng].append(op)
        stats = {e: 0 for e in ENGINES}

        def run_engine(ename, eng):
            waited = {}
            for op in per_eng[ename]:
                need = {}
                for d in op.deps:
                    if d.count is None or skip(d, op):
                        continue
                    if need.get(d.sem, 0) < d.count:
                        need[d.sem] = d.count
                for s, v in need.items():
                    if waited.get(s, 0) >= v:
                        continue
                    eng.wait_ge(sems[s], v)
                    waited[s] = v
                    stats[ename] += 1
                r = op.fn(eng)
                if op.dma_sem is not None:
                    assert len(r) == op.ndma, (op.name, len(r), op.ndma)
                    for ins in r:
                        ins.then_inc(sems[op.dma_sem], 16)
                elif op.need_sig:
                    assert r is not None, op.name
                    r.then_inc(sems[op.sem], 1)

        with nc.Block() as block:
            @block.tensor
            def _(e):
                run_engine("tensor", e)

            @block.scalar
            def _(e):
                run_engine("scalar", e)

            @block.vector
            def _(e):
                run_engine("vector", e)

            @block.gpsimd
            def _(e):
                run_engine("gpsimd", e)

            @block.sync
            def _(e):
                run_engine("sync", e)
        self.wait_stats = stats


ATT_Q, ATT_KV, RET_QK, RET_V = 512, 128, 512, 1024
OFF_AQ = 0
OFF_AK = 512
OFF_AV = 640
OFF_RQ = 768
OFF_RK = 1280
OFF_RV = 1792
OFF_RG = 2816
OFF_GA = 3840
OFF_GB = 4864

CHUNKS = [("A1", 8, 512), ("A2", 8, 384), ("GA1", 8, 512), ("GA2", 8, 512), ("WA", 4, 1024)]
for _h in range(4):
    CHUNKS += [("RA%d" % _h, 8, 512), ("RB%d" % _h, 8, 256)]
CHUNKS += [("GB1", 8, 512), ("GB2", 8, 512), ("WB1", 8, 512), ("WB2", 8, 512), ("WO1", 8, 512), ("WO2", 8, 512)]
for _q in range(4):
    CHUNKS += [("UP%da" % _q, 8, 512), ("UP%db" % _q, 8, 512), ("DN%da" % _q, 8, 512), ("DN%db" % _q, 8, 512)]
CH_SIZE = [kc * n for (_, kc, n) in CHUNKS]
CH_OFF = [int(v) for v in np.concatenate([[0], np.cumsum(CH_SIZE)[:-1]])]
CH_IDX = {name: i for i, (name, _, _) in enumerate(CHUNKS)}
WTOT = int(sum(CH_SIZE))
NCH = len(CHUNKS)

C_ID, C_MGT, C_MLE, C_GMIX, C_GMLP, C_SINK, C_DQ, C_NDQ, C_DK, C_NDK, C_GF = 0, 128, 256, 384, 400, 416, 432, 436, 440, 444, 448
NCONST = 448 + 1024
C_EPSQ = C_NDQ
NTAB = 192


def _chunk(W, cols, kc_n):
    sub = W[:, cols]
    return np.ascontiguousarray(sub.reshape(kc_n, 128, len(cols)).transpose(1, 0, 2)).reshape(128, -1)


def prep_weights(w_in, w_a, w_b, w_out, w_up, w_down):
    L = w_in.shape[0]
    out = np.empty((L, 128, WTOT), np.float32)
    perm = np.concatenate([np.arange(0, 128, 2), np.arange(1, 128, 2)])
    ar = np.arange
    for l in range(L):
        parts = {}
        parts["A1"] = _chunk(w_in[l], OFF_AQ + ar(512), 8)
        kcols = np.concatenate([OFF_AK + ar(64), OFF_AK + ar(64), OFF_AK + 64 + ar(64), OFF_AK + 64 + ar(64),
                                OFF_AV + ar(128)])
        parts["A2"] = _chunk(w_in[l], kcols, 8)
        parts["GA1"] = _chunk(w_in[l], OFF_GA + ar(512), 8)
        parts["GA2"] = _chunk(w_in[l], OFF_GA + 512 + ar(512), 8)
        parts["WA"] = _chunk(w_a[l], ar(1024), 4)
        for h in range(4):
            cols = np.concatenate([OFF_RQ + h * 128 + perm, OFF_RK + h * 128 + perm, OFF_RV + h * 256 + ar(256)])
            parts["RA%d" % h] = _chunk(w_in[l], cols, 8)
            parts["RB%d" % h] = _chunk(w_in[l], OFF_RG + h * 256 + ar(256), 8)
        parts["GB1"] = _chunk(w_in[l], OFF_GB + ar(512), 8)
        parts["GB2"] = _chunk(w_in[l], OFF_GB + 512 + ar(512), 8)
        parts["WB1"] = _chunk(w_b[l], ar(512), 8)
        parts["WB2"] = _chunk(w_b[l], 512 + ar(512), 8)
        parts["WO1"] = _chunk(w_out[l], ar(512), 8)
        parts["WO2"] = _chunk(w_out[l], 512 + ar(512), 8)
        for q in range(4):
            parts["UP%da" % q] = _chunk(w_up[l], q * 1024 + ar(512), 8)
            parts["UP%db" % q] = _chunk(w_up[l], q * 1024 + 512 + ar(512), 8)
            parts["DN%da" % q] = _chunk(w_down[l][q * 1024:(q + 1) * 1024], ar(512), 8)
            parts["DN%db" % q] = _chunk(w_down[l][q * 1024:(q + 1) * 1024], 512 + ar(512), 8)
        for i, (name, kc, n) in enumerate(CHUNKS):
            out[l, :, CH_OFF[i]:CH_OFF[i] + CH_SIZE[i]] = parts[name]
    return out


def prep_consts(g_mix, sinks, g_mlp, g_final):
    c = np.zeros((128, NCONST), np.float32)
    c[:, C_ID:C_ID + 128] = np.eye(128, dtype=np.float32)
    k = np.arange(128)[:, None]
    q = np.arange(128)[None, :]
    c[:, C_MGT:C_MGT + 128] = (k > q).astype(np.float32)
    c[:, C_MLE:C_MLE + 128] = (k <= q).astype(np.float32)
    L = g_mix.shape[0]
    for l in range(L):
        c[:, C_GMIX + l * 8:C_GMIX + l * 8 + 8] = g_mix[l].reshape(8, 128).T
        c[:, C_GMLP + l * 8:C_GMLP + l * 8 + 8] = g_mlp[l].reshape(8, 128).T
        for g in range(2):
            for half in range(2):
                for cc in range(2):
                    head = 4 * g + 2 * cc + half
                    c[:, C_SINK + l * 8 + g * 4 + half * 2 + cc] = sinks[l, head]
    i = np.arange(128, dtype=np.float64)
    for h in range(4):
        gam = 1.0 - 2.0 ** (-5.0 - h)
        dq = gam ** (i + 1.0)
        dk = gam ** (-(i + 1.0)) * (128.0 ** -0.5)
        c[:, C_DQ + h] = dq
        c[:, C_EPSQ + h] = EPS / (dq * dq)
        c[:, C_DK + h] = dk
        c[:, C_NDK + h] = -dk
    c[:, C_GF:C_GF + 1024] = g_final[None, :]
    return c


def prep_tabs(S=4096):
    pos = np.arange(S, dtype=np.float32)
    inv = (10000.0 ** (-np.arange(32, dtype=np.float32) / 32)).astype(np.float32)
    angA = pos[:, None] * inv[None, :]
    theta = (1.0 / (10000.0 ** np.linspace(0.0, 1.0, 64, dtype=np.float32))).astype(np.float32)
    angR = pos[:, None] * theta[None, :]
    t = np.concatenate([np.cos(angA), np.sin(angA), np.cos(angR), np.sin(angR)], axis=1)
    return np.ascontiguousarray(t.astype(np.float32))


GAM128 = [float((1.0 - 2.0 ** (-5.0 - h)) ** 128) for h in range(4)]


def build_program(nblocks=4, nlayers=2, stop=None, dumps=()):
    nc = bass.Bass("TRN2", target_bir_lowering=False)
    S = nblocks * TBLK
    x_d = nc.dram_tensor("x", [S, D], F32, kind="ExternalInput").ap()
    w_d = nc.dram_tensor("wts", [2, 128, WTOT], F32, kind="ExternalInput").ap()
    c_d = nc.dram_tensor("consts", [128, NCONST], F32, kind="ExternalInput").ap()
    t_d = nc.dram_tensor("tabs", [4096, NTAB], F32, kind="ExternalInput").ap()
    o_d = nc.dram_tensor("out", [S, D], F32, kind="ExternalOutput").ap()

    es = ExitStack()

    def sb(name, shape, dt):
        return es.enter_context(nc.sbuf_tensor("s_" + name, shape, dt))

    cst = sb("cst", [128, NCONST], F32)
    identb = sb("identb", [128, 128], BF16)
    mask2b = sb("mask2b", [128, 2, 128], BF16)
    esink = sb("esink", [128, 16], F32)
    mhalf = sb("mhalf", [128, 8], F32)
    dummy = sb("dummyk", [128, 8], F32)
    xres = sb("xres", [128, NT, D], F32)
    hT = sb("hT", [128, 8, TBLK], BF16)
    wsl = sb("wsl", [128, 4, 4096], BF16)
    tabs = sb("tabs", [128, NT, NTAB], F32)
    xn = sb("xn", [128, 2, D], BF16)
    junk = sb("junk", [128, D], BF16)
    scr = sb("scr", [128, 2, 512], F32)
    qrope = sb("qrope", [128, 2, 512], BF16)
    krope = sb("krope", [128, 2, 256], BF16)
    g1 = sb("g1", [128, 16384], BF16)
    KTs = sb("KTs", [128, 2, 2, 128], BF16)
    Vs = sb("Vs", [128, 2, 2, 65], BF16)
    Ebuf = sb("Ebuf", [128, 2, 2, 512], BF16)
    Atok = sb("Atok", [128, 2, 512], BF16)
    small = sb("small", [128, 64], F32)
    sgu = sb("sgu", [128, 8, TBLK], BF16)
    m1T = sb("m1T", [128, 8, TBLK], BF16)
    Smb = sb("Smb", [128, 2, 128], BF16)
    rtok = sb("rtok", [128, 2, 256], BF16)
    Z = sb("Z", [128, 2, 4, 256], F32)
    Rb = sb("Rb", [128, 2, 256], BF16)
    rT = sb("rT", [128, 8, TBLK], BF16)
    psf = es.enter_context(nc.psum_tensor("psf", [128, 6, 512], F32))
    psb = es.enter_context(nc.psum_tensor("psb", [128, 2, 1024], BF16))

    QT = g1[:, 0:4096].rearrange("p (c t) -> p c t", c=4)
    AT = g1[:, 4096:8192].rearrange("p (c t) -> p c t", c=4)
    KT = g1[:, 8192:10240].rearrange("p (g t) -> p g t", g=2)
    Vaug = g1[:, 10240:10240 + 8 * 2 * 65].rearrange("p (t g e) -> p t g e", t=8, g=2)
    qkT = g1[:, 0:4096].rearrange("p (a b t) -> p a b t", a=2, b=2)
    qktok = g1[:, 4096:8192].rearrange("p (a t d) -> p a t d", a=2, t=8)
    vtok = g1[:, 8192:12288].rearrange("p (a t d) -> p a t d", a=2, t=8)
    gate = g1[:, 12288:16384].rearrange("p (a t d) -> p a t d", a=2, t=8)
    G1K = ("g1",)

    P = Prog(nc)

    def op(eng, method, reads, writes, name="", **kw):
        return P.add(eng, lambda e: getattr(e, method)(**kw), reads, writes, name)

    def mm(items, reads, writes, name=""):
        def fn(e):
            ins = None
            for (o, l, r, st_, sp) in items:
                ins = e.matmul(o, l, r, start=st_, stop=sp)
            return ins
        return P.add("tensor", fn, reads, writes, name)

    def tr(items, reads, writes, name=""):
        def fn(e):
            ins = None
            for (o, i_) in items:
                ins = e.transpose(o, i_, identb[:])
            return ins
        return P.add("tensor", fn, reads + ["identb"], writes, name)

    st = {"pf": 0, "pb": 0, "sm": 0, "scr": 0}

    def pf():
        i = st["pf"]
        st["pf"] = (i + 1) % 6
        return psf[:, i, :], ("pf", i)

    def pb():
        i = st["pb"]
        st["pb"] = (i + 1) % 2
        return psb[:, i, :], ("pb", i)

    def sm(n):
        i = st["sm"]
        if i + n > 64:
            i = 0
        st["sm"] = i + n
        return small[:, i:i + n], [("sm", j) for j in range(i, i + n)]

    seq = [(l, ci) for b in range(nblocks) for l in range(nlayers) for ci in range(NCH)]
    wst = {"next_load": 0, "next_rel": 0, "done": set()}

    def w_issue():
        n = wst["next_load"]
        if n >= len(seq):
            return
        wst["next_load"] = n + 1
        l, ci = seq[n]
        slot = n % 4
        size = CH_SIZE[ci]
        src = w_d[l, :, CH_OFF[ci]:CH_OFF[ci] + size]
        dst = wsl[:, slot, 0:size]
        P.dma("gpsimd", lambda e: [e.dma_start(out=dst, in_=src, max_dma_last_dim=4096)], "w%d" % slot, 1,
              writes=[("w", slot)], name="wload%d" % n)

    def w_get(name, b, l):
        n = (b * nlayers + l) * NCH + CH_IDX[name]
        assert n < wst["next_load"], ("chunk not loaded yet (ring too small for this interleaving)", name, n)
        assert n >= wst["next_rel"]
        slot = n % 4
        kc, ncols = CHUNKS[CH_IDX[name]][1], CHUNKS[CH_IDX[name]][2]
        view = wsl[:, slot, 0:kc * ncols].rearrange("p (k n) -> p k n", k=kc)
        return view, ("w", slot), n

    def w_done(n):
        wst["done"].add(n)
        while wst["next_rel"] in wst["done"]:
            wst["done"].remove(wst["next_rel"])
            wst["next_rel"] += 1
            w_issue()

    P.dma("sync", lambda e: [e.dma_start(out=cst[:], in_=c_d[:, :])], "cstl", 1, writes=["cst"])
    op("vector", "tensor_copy", ["cst"], ["identb"], out=identb[:], in_=cst[:, C_ID:C_ID + 128])
    op("vector", "tensor_copy", ["cst"], ["mask2b"], out=mask2b[:, 0, :], in_=cst[:, C_MGT:C_MGT + 128])
    op("vector", "tensor_copy", ["cst", "mask2b"], ["mask2b"], out=mask2b[:, 1, :], in_=cst[:, C_MLE:C_MLE + 128])
    op("scalar", "activation", ["cst"], ["esink"], out=esink[:], in_=cst[:, C_SINK:C_SINK + 16], func=AF.Exp)
    op("vector", "memset", [], [("Z", l, h) for l in range(2) for h in range(4)], ap=Z[:], constant=0.0)
    op("gpsimd", "memset", [], ["mhalf"], ap=mhalf[:], constant=-0.5)
    for _ in range(4):
        w_issue()

    xkeys = lambda t: [("x", t, 0), ("x", t, 1)]

    def rstd_from_ms(ms_ap, ms_keys, eps_ap=None):
        v, vk = sm(1)
        r, rk = sm(1)
        if eps_ap is None:
            op("gpsimd", "tensor_scalar", ms_keys, vk, out=v, in0=ms_ap, scalar1=EPS, scalar2=None, op0=ALU.add)
        else:
            op("gpsimd", "tensor_tensor", ms_keys + ["cst"], vk, out=v, in0=ms_ap, in1=eps_ap, op=ALU.add)
        op("gpsimd", "tensor_tensor", vk + ["mhalf"], rk, out=r, in0=v, in1=mhalf[:, 0:1], op=ALU.pow)
        return r, rk

    def norm_pre(t):
        ms, msk = sm(1)
        op("scalar", "activation", xkeys(t), msk, out=junk[:], in_=xres[:, t, :], func=AF.Square,
           scale=1.0 / 32.0, accum_out=ms)
        r, rk = rstd_from_ms(ms, msk)
        par = t % 2
        op("scalar", "activation", xkeys(t) + rk, [("xn", par)], out=xn[:, par, :], in_=xres[:, t, :], func=AF.Copy,
           scale=r)

    def norm_post(t, gcol):
        par = t % 2
        bank, bk = pb()
        tr([(bank[:, kc * 128:(kc + 1) * 128], xn[:, par, kc * 128:(kc + 1) * 128]) for kc in range(8)],
           [("xn", par)], [bk])
        gb = cst[:, gcol:gcol + 8].unsqueeze(2).broadcast_to([128, 8, 128])
        op("vector", "tensor_tensor", [bk, "cst"], [("hT", t)], out=hT[:, :, t * 128:(t + 1) * 128],
           in0=bank.rearrange("p (k t) -> p k t", k=8), in1=gb, op=ALU.mult)

    def rope(xa, nh, half, cosb, sinb, sc, out_ap, rk, wk, scr_off):
        n = nh * 2 * half
        u = scr[:, 0, scr_off:scr_off + n]
        tt = scr[:, 1, scr_off:scr_off + n]
        xv = xa.rearrange("p (h two d) -> p h two d", two=2, d=half)
        uv = u.rearrange("p (h two d) -> p h two d", two=2, d=half)
        tv = tt.rearrange("p (h two d) -> p h two d", two=2, d=half)
        sbc = cosb.unsqueeze(1).broadcast_to([128, nh, half])
        sbb = sinb.unsqueeze(1).broadcast_to([128, nh, half])
        offs = [o for o in (0, 128) if (o == scr_off) or (scr_off < o < scr_off + n)]
        ku = [("scr", 0, o) for o in offs]
        kt_ = [("scr", 1, o) for o in offs]
        for j in range(2):
            op("vector", "scalar_tensor_tensor", rk + ku, ku, out=uv[:, :, j, :], in0=xv[:, :, j, :], scalar=sc, in1=sbc,
               op0=ALU.mult, op1=ALU.mult)
        op("vector", "scalar_tensor_tensor", rk + kt_, kt_, out=tv[:, :, 0, :], in0=xv[:, :, 1, :],
           scalar=-sc, in1=sbb, op0=ALU.mult, op1=ALU.mult)
        op("vector", "scalar_tensor_tensor", rk + kt_, kt_, out=tv[:, :, 1, :], in0=xv[:, :, 0, :],
           scalar=sc, in1=sbb, op0=ALU.mult, op1=ALU.mult)
        op("vector", "tensor_tensor", ku + kt_, wk, out=out_ap, in0=u, in1=tt, op=ALU.add)

    def barrier_g1():
        op("vector", "memset", [], [G1K], ap=dummy[:, 0:1], constant=0.0)

    def add_x(bank, bk, t, ch, scale=None):
        if scale is None:
            op("vector", "tensor_tensor", [bk, ("x", t, ch)], [("x", t, ch)], out=xres[:, t, ch * 512:(ch + 1) * 512],
               in0=bank[:, :], in1=xres[:, t, ch * 512:(ch + 1) * 512], op=ALU.add)
        else:
            op("vector", "scalar_tensor_tensor", [bk, ("x", t, ch)], [("x", t, ch)],
               out=xres[:, t, ch * 512:(ch + 1) * 512], in0=bank[:, :], scalar=scale,
               in1=xres[:, t, ch * 512:(ch + 1) * 512], op0=ALU.mult, op1=ALU.add)

    def gate_group(wg, kg, base, fc, half):
        bank, bk = pf()
        mm([(bank[:, :], wg[:, kc, fc * 128:(fc + 1) * 128], hT[:, kc, half * 512:(half + 1) * 512],
             kc == 0, kc == 7) for kc in range(8)],
           [("hT", tt_) for tt_ in range(half * 4, half * 4 + 4)] + [kg], [bk])
        op("scalar", "activation", [bk], [("sgu", base + fc, half)],
           out=sgu[:, base + fc, half * 512:(half + 1) * 512], in_=bank[:, :], func=AF.Tanh, scale=0.5)

    def layer(b, l, norm_inline, tail_kind, next_gcol):
        first_blk = (b == 0)
        gcol1 = C_GMIX + l * 8
        barrier_g1()
        op("vector", "memset", [G1K], [("V", t) for t in range(NT)], ap=Vaug[:, :, :, 64:65], constant=1.0)
        wA1, kA1, nA1 = w_get("A1", b, l)
        pend = []
        if norm_inline:
            norm_pre(0)
            norm_pre(1)
        for t in range(NT):
            if norm_inline:
                norm_post(t, gcol1)
                if t + 2 < NT:
                    norm_pre(t + 2)
            bank, bk = pf()
            mm([(bank[:, 0:512], hT[:, kc, t * 128:(t + 1) * 128], wA1[:, kc, :], kc == 0, kc == 7) for kc in range(8)],
               [("hT", t), kA1], [bk])
            par = t % 2
            rope(bank[:, 0:512], 8, 32, tabs[:, t, 0:32], tabs[:, t, 32:64], 0.125, qrope[:, par, :],
                 [bk, "tabs"], [("qrope", par)], 0)

            def qtr(t=t, par=par):
                tb, tbk = pb()
                tr([(tb[:, c * 128:(c + 1) * 128], qrope[:, par, c * 128:(c + 1) * 128]) for c in range(4)],
                   [("qrope", par)], [tbk])
                op("scalar", "activation", [tbk, G1K], [("QT", t)], out=QT[:, :, t * 128:(t + 1) * 128],
                   in_=tb[:, 0:512].rearrange("p (c t) -> p c t", c=4), func=AF.Copy)
            if pend:
                pend.pop(0)()
            pend.append(qtr)
        w_done(nA1)
        wA2, kA2, nA2 = w_get("A2", b, l)
        for t in range(NT):
            bank, bk = pf()
            mm([(bank[:, 0:384], hT[:, kc, t * 128:(t + 1) * 128], wA2[:, kc, :], kc == 0, kc == 7) for kc in range(8)],
               [("hT", t), kA2], [bk])
            par = t % 2
            rope(bank[:, 0:256], 4, 32, tabs[:, t, 0:32], tabs[:, t, 32:64], 1.0, krope[:, par, :],
                 [bk, "tabs"], [("krope", par)], 0)
            op("scalar", "activation", [bk, G1K, ("V", t)], [("V", t)], out=Vaug[:, t, :, 0:64],
               in_=bank[:, 256:384].rearrange("p (g d) -> p g d", g=2), func=AF.Copy)

            def ktr(t=t, par=par):
                tb, tbk = pb()
                tr([(tb[:, g * 128:(g + 1) * 128], krope[:, par, g * 128:(g + 1) * 128]) for g in range(2)],
                   [("krope", par)], [tbk])
                op("scalar", "activation", [tbk, G1K], [("KT", t)], out=KT[:, :, t * 128:(t + 1) * 128],
                   in_=tb[:, 0:256].rearrange("p (g t) -> p g t", g=2), func=AF.Copy)
            if pend:
                pend.pop(0)()
            pend.append(ktr)
        w_done(nA2)
        while pend:
            pend.pop(0)()
        units = [(t, g) for t in range(NT) for g in range(2)]

        def stage_S(t, g, u):
            noprev = first_blk and t == 0
            kbs = [1] if noprev else [0, 1]
            banks = []
            for half in range(2):
                bank, bk = pf()
                banks.append((bank, bk))
                items = []
                reads = [("QT", t), G1K]
                for kb in kbs:
                    if kb == 1:
                        ksrc = KT[half * 64:(half + 1) * 64, g, t * 128:(t + 1) * 128]
                        reads.append(("KT", t))
                    elif t > 0:
                        ksrc = KT[half * 64:(half + 1) * 64, g, (t - 1) * 128:t * 128]
                        reads.append(("KT", t - 1))
                    else:
                        ksrc = KTs[half * 64:(half + 1) * 64, l, g, :]
                        reads.append(("KTs", l))
                    items.append((bank[:, kb * 256:(kb + 1) * 256].rearrange("p (c q) -> p c q", c=2), ksrc,
                                  QT[half * 64:(half + 1) * 64, 2 * g:2 * g + 2, t * 128:(t + 1) * 128], True, True))
                mm(items, reads, [bk])
            ue = u % 2
            for half in range(2):
                bank, bk = banks[half]
                lo = 256 if noprev else 0
                op("scalar", "activation", [bk], [("E", ue, half)], out=Ebuf[:, ue, half, lo:512], in_=bank[:, lo:512],
                   func=AF.Exp)
                if noprev:
                    ev = Ebuf[:, ue, half, 256:512].rearrange("p (c q) -> p c q", c=2)
                    mk = mask2b[:, 1, :].unsqueeze(1).broadcast_to([128, 2, 128])
                else:
                    ev = Ebuf[:, ue, half, :].rearrange("p (k c q) -> p k c q", k=2, c=2)
                    mk = mask2b[:, :, :].unsqueeze(2).broadcast_to([128, 2, 2, 128])
                op("vector", "tensor_tensor", [("E", ue, half), "mask2b"], [("E", ue, half)], out=ev, in0=ev, in1=mk,
                   op=ALU.mult)

        def stage_PV(t, g, u):
            noprev = first_blk and t == 0
            kbs = [1] if noprev else [0, 1]
            ue = u % 2
            bank, bk = pf()
            items = []
            reads = [("E", ue, 0), ("E", ue, 1), G1K]
            for half in range(2):
                for cc in range(2):
                    jj = half * 2 + cc
                    for i_, kb in enumerate(kbs):
                        if kb == 1:
                            vsrc = Vaug[:, t, g, :]
                            reads.append(("V", t))
                        elif t > 0:
                            vsrc = Vaug[:, t - 1, g, :]
                            reads.append(("V", t - 1))
                        else:
                            vsrc = Vs[:, l, g, :]
                            reads.append(("Vs", l))
                        items.append((bank[:, jj * 65:(jj + 1) * 65],
                                      Ebuf[:, ue, half, kb * 256 + cc * 128:kb * 256 + (cc + 1) * 128], vsrc,
                                      i_ == 0, i_ == len(kbs) - 1))
            mm(items, reads, [bk])
            ov = bank[:, 0:260].rearrange("p (h c e) -> p h c e", h=2, c=2)
            den, dk_ = sm(4)
            rden, rk_ = sm(4)
            op("vector", "tensor_tensor", [bk, "esink"], dk_, out=den.rearrange("p (h c) -> p h c", h=2),
               in0=ov[:, :, :, 64], in1=esink[:, l * 8 + g * 4:l * 8 + g * 4 + 4].rearrange("p (h c) -> p h c", h=2),
               op=ALU.add)
            op("vector", "reciprocal", dk_, rk_, out=rden, in_=den)
            ap_ = t % 2
            ao = Atok[:, ap_, g * 256:(g + 1) * 256].rearrange("p (c h d) -> p h c d", c=2, h=2)
            rb_ = rden.rearrange("p (h c) -> p h c", h=2).unsqueeze(3).broadcast_to([128, 2, 2, 64])
            op("vector", "tensor_tensor", [bk] + rk_, [("Atok", ap_, g)], out=ao, in0=ov[:, :, :, 0:64], in1=rb_,
               op=ALU.mult)

        def stage_T(t):
            ap_ = t % 2
            tb, tbk = pb()
            tr([(tb[:, c * 128:(c + 1) * 128], Atok[:, ap_, c * 128:(c + 1) * 128]) for c in range(4)],
               [("Atok", ap_, 0), ("Atok", ap_, 1)], [tbk])
            op("scalar", "activation", [tbk, G1K], [("AT", t)], out=AT[:, :, t * 128:(t + 1) * 128],
               in_=tb[:, 0:512].rearrange("p (c t) -> p c t", c=4), func=AF.Copy)

        ga_groups = [(nm, base, fc, half) for (nm, base) in (("GA1", 0), ("GA2", 4)) for fc in range(4) for half in range(2)]
        ga_w = {}
        for i in range(len(units) + 3):
            if i < len(units):
                stage_S(units[i][0], units[i][1], i)
            if i < len(ga_groups):
                nm, base, fc, half = ga_groups[i]
                if nm not in ga_w:
                    ga_w[nm] = w_get(nm, b, l)
                wg, kg, ng = ga_w[nm]
                gate_group(wg, kg, base, fc, half)
                if fc == 3 and half == 1:
                    w_done(ng)
            j = i - 1
            if 0 <= j < len(units):
                stage_PV(units[j][0], units[j][1], j)
            k = i - 3
            if 0 <= k < len(units) and units[k][1] == 1:
                stage_T(units[k][0])
        op("vector", "tensor_copy", [("KT", 7), G1K], [("KTs", l)], out=KTs[:, l, :, :], in_=KT[:, :, 7 * 128:8 * 128])
        op("vector", "tensor_copy", [("V", 7), G1K], [("Vs", l)], out=Vs[:, l, :, :], in_=Vaug[:, 7, :, :])
        wa, ka, na = w_get("WA", b, l)
        for half in range(2):
            for c in range(8):
                bank, bk = pf()
                mm([(bank[:, :], wa[:, kc, c * 128:(c + 1) * 128], AT[:, kc, half * 512:(half + 1) * 512],
                     kc == 0, kc == 3) for kc in range(4)],
                   [("AT", tt_) for tt_ in range(half * 4, half * 4 + 4)] + [ka, G1K], [bk])
                op("vector", "scalar_tensor_tensor", [bk, ("sgu", c, half)], [("m1T", c, half)],
                   out=m1T[:, c, half * 512:(half + 1) * 512], in0=sgu[:, c, half * 512:(half + 1) * 512], scalar=1.0,
                   in1=bank[:, :], op0=ALU.add, op1=ALU.mult)
        w_done(na)
        barrier_g1()
        wts_h = {}

        def A_ra(h, t):
            par = h % 2
            if t == 0:
                wts_h[h] = (w_get("RA%d" % h, b, l), w_get("RB%d" % h, b, l))
            (wra, kra, _), _ = wts_h[h]
            bank, bk = pf()
            mm([(bank[:, 0:512], hT[:, kc, t * 128:(t + 1) * 128], wra[:, kc, :], kc == 0, kc == 7) for kc in range(8)],
               [("hT", t), kra], [bk])
            return bank, bk

        def A_rope(h, t, bank, bk):
            par = h % 2
            rope(bank[:, 0:256], 2, 64, tabs[:, t, 64:128], tabs[:, t, 128:192], 1.0, qktok[:, par, t, :],
                 [bk, "tabs", G1K], [("qktok", par, t)], 128)
            op("scalar", "activation", [bk, G1K, "cst"], [("vtok", par, t)], out=vtok[:, par, t, :], in_=bank[:, 256:512],
               func=AF.Copy, scale=cst[:, C_DK + h:C_DK + h + 1])

        def A_rb(h, t):
            par = h % 2
            _, (wrb, krb, _) = wts_h[h]
            bank, bk = pf()
            mm([(bank[:, 0:256], hT[:, kc, t * 128:(t + 1) * 128], wrb[:, kc, :], kc == 0, kc == 7) for kc in range(8)],
               [("hT", t), krb], [bk])
            op("scalar", "activation", [bk, G1K], [("gate", par, t)], out=gate[:, par, t, :], in_=bank[:, 0:256],
               func=AF.Silu)
            if t == NT - 1:
                w_done(wts_h[h][0][2])
                w_done(wts_h[h][1][2])

        def A_tr(h, t):
            par = h % 2
            tb, tbk = pb()
            tr([(tb[:, 0:128], qktok[:, par, t, 0:128]), (tb[:, 128:256], qktok[:, par, t, 128:256])],
               [("qktok", par, t), G1K], [tbk])
            op("scalar", "activation", [tbk, G1K], [("qkT", par, t)], out=qkT[:, par, :, t * 128:(t + 1) * 128],
               in_=tb[:, 0:256].rearrange("p (a t) -> p a t", a=2), func=AF.Copy)

        def B_s_kv(h, t, pend_norm):
            par = h % 2
            first = first_blk and t == 0
            sp = (h * 8 + t) % 2
            bank, bk = pf()
            mm([(bank[:, 0:128], qkT[:, par, 1, t * 128:(t + 1) * 128], qkT[:, par, 0, t * 128:(t + 1) * 128], True, True)],
               [("qkT", par, t), G1K], [bk])
            kvb, kvk = pf()
            mm([(kvb[:, 0:256], qktok[:, par, t, 128:256], vtok[:, par, t, :], True, True)],
               [("qktok", par, t), ("vtok", par, t), G1K], [kvk])
            op("vector", "tensor_tensor", [bk, "cst"], [("Sm", sp)], out=Smb[:, sp, :], in0=bank[:, 0:128],
               in1=cst[:, C_MLE:C_MLE + 128], op=ALU.mult)
            if (not first) and t == 0:
                op("scalar", "activation", [("Z", l, h)], [("Rb", 0)], out=Rb[:, 0, :], in_=Z[:, l, h, :], func=AF.Copy,
                   scale=GAM128[h])
            op("vector", "scalar_tensor_tensor", [kvk, ("Z", l, h)], [("Z", l, h)], out=Z[:, l, h, :], in0=Z[:, l, h, :],
               scalar=GAM128[h], in1=kvb[:, 0:256], op0=ALU.mult, op1=ALU.add)
            if t < NT - 1:
                nb_ = (t + 1) % 2
                op("scalar", "activation", [("Z", l, h)], [("Rb", nb_)], out=Rb[:, nb_, :], in_=Z[:, l, h, :],
                   func=AF.Copy, scale=GAM128[h])
            while pend_norm:
                pend_norm.pop(0)()
            return sp, kvb, kvk

        pend_norm = []

        def B_r(h, t, sp, kvb, kvk):
            par = h % 2
            first = first_blk and t == 0
            rbp = t % 2
            bank, bk = pf()
            items = [(bank[:, 0:256], Smb[:, sp, :], vtok[:, par, t, :], True, first)]
            reads = [("Sm", sp), ("vtok", par, t), G1K]
            if not first:
                items.append((bank[:, 0:256], qkT[:, par, 0, t * 128:(t + 1) * 128], Rb[:, rbp, :], False, True))
                reads += [("qkT", par, t), ("Rb", rbp)]
            mm(items, reads, [bk])
            ms, msk = sm(1)
            op("scalar", "activation", [bk], msk, out=junk[:, 0:256], in_=bank[:, 0:256], func=AF.Square,
               scale=1.0 / 16.0, accum_out=ms)
            r, rk = rstd_from_ms(ms, msk, eps_ap=cst[:, C_EPSQ + h:C_EPSQ + h + 1])
            rp = (h * 8 + t) % 2

            def norm_fn():
                op("vector", "scalar_tensor_tensor", [bk, ("gate", par, t), G1K] + rk, [("rtok", rp)],
                   out=rtok[:, rp, :], in0=bank[:, 0:256], scalar=r, in1=gate[:, par, t, :], op0=ALU.mult,
                   op1=ALU.mult)
            pend_norm.append(norm_fn)
            return rp

        def B_tr(h, t, rp):
            tb, tbk = pb()
            tr([(tb[:, 0:128], rtok[:, rp, 0:128]), (tb[:, 128:256], rtok[:, rp, 128:256])], [("rtok", rp)], [tbk])
            op("scalar", "activation", [tbk], [("rT", h, t)], out=rT[:, 2 * h:2 * h + 2, t * 128:(t + 1) * 128],
               in_=tb[:, 0:256].rearrange("p (a t) -> p a t", a=2), func=AF.Copy)

        gb_groups = [(nm, base, fc, half) for (nm, base) in (("GB1", 0), ("GB2", 4)) for fc in range(4) for half in range(2)]
        gb_w = {}

        def gb_fill(i):
            nm, base, fc, half = gb_groups[i]
            if nm not in gb_w:
                gb_w[nm] = w_get(nm, b, l)
            wg, kg, ng = gb_w[nm]
            gate_group(wg, kg, base, fc, half)
            if fc == 3 and half == 1:
                w_done(ng)

        pend_btr = []
        pend_atr = None
        for h in range(5):
            for t in range(NT):
                ab = None
                if h < 4:
                    ab = A_ra(h, t)
                else:
                    gb_fill(2 * t)
                bs = None
                if h >= 1:
                    bs = B_s_kv(h - 1, t, pend_norm)
                if len(pend_btr) >= 2:
                    B_tr(*pend_btr.pop(0))
                if h < 4:
                    A_rb(h, t)
                    A_rope(h, t, *ab)
                else:
                    gb_fill(2 * t + 1)
                if h >= 1:
                    rp = B_r(h - 1, t, *bs)
                    pend_btr.append((h - 1, t, rp))
                if pend_atr is not None:
                    A_tr(*pend_atr)
                    pend_atr = None
                if h < 4:
                    pend_atr = (h, t)
                    if t == NT - 1:
                        A_tr(*pend_atr)
                        pend_atr = None
        while pend_norm:
            pend_norm.pop(0)()
        for nm, base in (("WB1", 0), ("WB2", 4)):
            wb_, kb_, nb_w = w_get(nm, b, l)
            for half in range(2):
                for fc in range(4):
                    c = base + fc
                    bank, bk = pf()
                    mm([(bank[:, :], wb_[:, kc, fc * 128:(fc + 1) * 128], rT[:, kc, half * 512:(half + 1) * 512],
                         kc == 0, kc == 7) for kc in range(8)],
                       [("rT", hh, tt_) for hh in range(4) for tt_ in range(half * 4, half * 4 + 4)] + [kb_], [bk])
                    while pend_btr:
                        B_tr(*pend_btr.pop(0))
                    sp_ = st["scr"]
                    st["scr"] = 1 - sp_
                    skey = ("scr", sp_, 0)
                    op("vector", "scalar_tensor_tensor", [bk, ("sgu", c, half)], [skey, ("scr", sp_, 128)],
                       out=scr[:, sp_, :], in0=sgu[:, c, half * 512:(half + 1) * 512], scalar=1.0, in1=bank[:, :],
                       op0=ALU.add, op1=ALU.mult)
                    op("vector", "tensor_tensor", [skey, ("m1T", c, half)], [("m1T", c, half)],
                       out=m1T[:, c, half * 512:(half + 1) * 512], in0=scr[:, sp_, :],
                       in1=m1T[:, c, half * 512:(half + 1) * 512], op=ALU.add)
            w_done(nb_w)
        gcol2 = C_GMLP + l * 8
        wo1, ko1, no1 = w_get("WO1", b, l)
        wo2, ko2, no2 = w_get("WO2", b, l)
        pend_post = []
        for t in range(NT):
            for ch, (wo, ko) in enumerate(((wo1, ko1), (wo2, ko2))):
                bank, bk = pf()
                mm([(bank[:, :], m1T[:, kc, t * 128:(t + 1) * 128], wo[:, kc, :], kc == 0, kc == 7) for kc in range(8)],
                   [("m1T", kc, t // 4) for kc in range(8)] + [ko], [bk])
                add_x(bank, bk, t, ch, scale=0.5)
            norm_pre(t)
            if pend_post:
                norm_post(pend_post.pop(0), gcol2)
            pend_post.append(t)
        w_done(no1)
        w_done(no2)
        for q in range(4):
            gi = 0
            for nm, base in (("UP%da" % q, 0), ("UP%db" % q, 4)):
                wu, ku, nu = w_get(nm, b, l)
                for half in range(2):
                    for fc in range(4):
                        bank, bk = pf()
                        mm([(bank[:, :], wu[:, kc, fc * 128:(fc + 1) * 128], hT[:, kc, half * 512:(half + 1) * 512],
                             kc == 0, kc == 7) for kc in range(8)],
                           [("hT", tt_) for tt_ in range(half * 4, half * 4 + 4)] + [ku], [bk])
                        gi += 1
                        if pend_post and gi == 2:
                            norm_post(pend_post.pop(0), gcol2)
                        sp_ = st["scr"]
                        st["scr"] = 1 - sp_
                        skey = ("scr", sp_, 0)
                        op("scalar", "activation", [bk], [skey, ("scr", sp_, 128)], out=scr[:, sp_, :], in_=bank[:, :],
                           func=AF.Copy)
                        op("vector", "scalar_tensor_tensor", [bk, skey], [("sgu", base + fc, half)],
                           out=sgu[:, base + fc, half * 512:(half + 1) * 512], in0=bank[:, :], scalar=0.0,
                           in1=scr[:, sp_, :], op0=ALU.max, op1=ALU.mult)
                w_done(nu)
            if q < 3:
                for nm, ch in (("DN%da" % q, 0), ("DN%db" % q, 1)):
                    wd, kd, nd = w_get(nm, b, l)
                    for t in range(NT):
                        bank, bk = pf()
                        mm([(bank[:, :], sgu[:, fc, t * 128:(t + 1) * 128], wd[:, fc, :], fc == 0, fc == 7)
                            for fc in range(8)], [("sgu", fc, t // 4) for fc in range(8)] + [kd], [bk])
                        add_x(bank, bk, t, ch)
                    w_done(nd)
            else:
                wd1, kd1, nd1 = w_get("DN3a", b, l)
                wd2, kd2, nd2 = w_get("DN3b", b, l)
                pend_n = []
                for t in range(NT):
                    for ch, (wd, kd) in enumerate(((wd1, kd1), (wd2, kd2))):
                        bank, bk = pf()
                        mm([(bank[:, :], sgu[:, fc, t * 128:(t + 1) * 128], wd[:, fc, :], fc == 0, fc == 7)
                            for fc in range(8)], [("sgu", fc, t // 4) for fc in range(8)] + [kd], [bk])
                        add_x(bank, bk, t, ch)
                    if tail_kind == "norm":
                        norm_pre(t)
                        if pend_n:
                            norm_post(pend_n.pop(0), next_gcol)
                        pend_n.append(t)
                    else:
                        ms, msk = sm(1)
                        op("scalar", "activation", xkeys(t), msk, out=junk[:], in_=xres[:, t, :], func=AF.Square,
                           scale=1.0 / 32.0, accum_out=ms)
                        r, rk = rstd_from_ms(ms, msk)
                        op("vector", "scalar_tensor_tensor", xkeys(t) + rk + ["cst"], xkeys(t), out=xres[:, t, :],
                           in0=xres[:, t, :], scalar=r, in1=cst[:, C_GF:C_GF + 1024], op0=ALU.mult, op1=ALU.mult)
                        r0 = (b * NT + t) * 128
                        P.dma("sync", lambda e, t=t, r0=r0: [e.dma_start(out=o_d[r0:r0 + 128, :], in_=xres[:, t, :])],
                              "xs%d" % t, 1, reads=xkeys(t), writes=[("out", b, t)])
                w_done(nd1)
                w_done(nd2)
                while pend_n:
                    norm_post(pend_n.pop(0), next_gcol)

    for b in range(nblocks):
        P.epoch = b
        for t in range(NT):
            r0 = (b * NT + t) * 128
            P.dma("sync", lambda e, t=t, r0=r0: [e.dma_start(out=xres[:, t, :], in_=x_d[r0:r0 + 128, :])], "xl%d" % t, 1,
                  writes=xkeys(t))
        P.dma("sync", lambda e, b=b: [e.dma_start(out=tabs[:], in_=t_d[b * TBLK:(b + 1) * TBLK, :].rearrange(
            "(t p) c -> p t c", p=128))], "tbl", 1, writes=["tabs"])
        for l in range(nlayers):
            last = (l == nlayers - 1)
            layer(b, l, norm_inline=(l == 0), tail_kind=("final" if last else "norm"),
                  next_gcol=(None if last else C_GMIX + (l + 1) * 8))
    outk = [k for k in P.last_writer.keys() if isinstance(k, tuple) and k[0] in ("out",)]
    outk += [("w", s_) for s_ in range(4)] + ["tabs", "cst"]
    P.add("sync", lambda e: None, reads=outk, writes=[], name="final")

    sem_names = P.sem_names()
    sems = {n: es.enter_context(nc.semaphore(n)) for n in sem_names}
    P.emit(sems)
    es.close()
    return nc, P


_CACHE = {}


def kernel(x, g_mix, w_in, sinks, w_a, w_b, w_out, g_mlp, w_up, w_down, g_final):
    x = np.asarray(x, np.float32)
    B = x.shape[0]
    wts = prep_weights(np.asarray(w_in, np.float32), np.asarray(w_a, np.float32), np.asarray(w_b, np.float32),
                       np.asarray(w_out, np.float32), np.asarray(w_up, np.float32), np.asarray(w_down, np.float32))
    consts = prep_consts(np.asarray(g_mix, np.float32), np.asarray(sinks, np.float32), np.asarray(g_mlp, np.float32),
                         np.asarray(g_final, np.float32))
    tabs = prep_tabs(4096)
    nc, _ = build_program(4, 2)
    in_maps = [{"x": np.ascontiguousarray(x[b]), "wts": wts, "consts": consts, "tabs": tabs} for b in range(B)]
    res = run_bass_kernel_spmd(nc, in_maps, core_ids=list(range(B)))
    out = np.stack([np.asarray(r["out"], np.float32) for r in res.results], axis=0)
    return out
```
